# Optimizing a Trainium2 kernel written in Bass

```python
import math
import jax, jax.numpy as jnp
from jax import lax
import numpy as np

D_MODEL = 1024
BATCH = 8
SEQ = 2048
DEPTH = 4

D_HEAD = 64
A_WIDTH = D_MODEL // 4
B_HEADS = (D_MODEL // 2) // D_HEAD
B_WIDTH = B_HEADS * D_HEAD
C_WIDTH = D_MODEL // 4
D_MIX = A_WIDTH + B_WIDTH + C_WIDTH
SPLIT_SIZES = (A_WIDTH, A_WIDTH, A_WIDTH,
               B_WIDTH, B_WIDTH, B_WIDTH,
               C_WIDTH, C_WIDTH)
IN_COLS = sum(SPLIT_SIZES)
DILATED_BRANCHES = ((128, 1), (512, 4), (2048, 16))
BLK = 128
NUM_BUCKETS = 32
MAX_DISTANCE = 2048
SHORT_CONV = 3
CONFORMER_CONV = 31
FFN_CONV = 3
D_FF = ((8 * D_MODEL // 3 + 127) // 128) * 128
EPS = 1e-6
NEG = -1e30

kernel_name = 'hybrid_shortconv_dilatedattn_conformer_trunk'


def rmsnorm(x, g):
    xf = x.astype(jnp.float32)
    y = xf * lax.rsqrt(jnp.mean(xf * xf, axis=-1, keepdims=True) + EPS)
    return (y * g.astype(jnp.float32)).astype(x.dtype)


def layernorm(x, g, b):
    xf = x.astype(jnp.float32)
    mu = jnp.mean(xf, axis=-1, keepdims=True)
    var = jnp.mean(jnp.square(xf - mu), axis=-1, keepdims=True)
    y = (xf - mu) * lax.rsqrt(var + EPS)
    return (y * g.astype(jnp.float32) + b.astype(jnp.float32)).astype(x.dtype)


def causal_dwconv(x, w):
    k_width, ch = w.shape
    return lax.conv_general_dilated(
        x, w[:, None, :].astype(x.dtype), window_strides=(1,),
        padding=[(k_width - 1, 0)], dimension_numbers=('NWC', 'WIO', 'NWC'),
        feature_group_count=ch)


def t5_bucket(dist):
    max_exact = NUM_BUCKETS // 2
    d_f = jnp.maximum(dist, 1).astype(jnp.float32)
    large = max_exact + (jnp.log(d_f / max_exact) / math.log(MAX_DISTANCE / max_exact)
                         * (NUM_BUCKETS - max_exact)).astype(jnp.int32)
    large = jnp.minimum(large, NUM_BUCKETS - 1)
    return jnp.where(dist < max_exact, dist, large)


def dilated_branch(q, k, v, rel_bias, window, dilation):
    bsz, seq, heads, hd = q.shape
    n_keys = window // dilation
    sub_len = seq // dilation
    n_blk = -(-sub_len // BLK)
    pad = n_blk * BLK - sub_len

    def to_blocks(t):
        t = t.reshape(bsz, sub_len, dilation, heads, hd).transpose(0, 2, 3, 1, 4)
        t = jnp.pad(t, ((0, 0), (0, 0), (0, 0), (0, pad), (0, 0)))
        return t.reshape(bsz, dilation, heads, n_blk, BLK, hd)

    qb, kb, vb = to_blocks(q), to_blocks(k), to_blocks(v)

    def with_prev(t):
        prev = jnp.concatenate([jnp.zeros_like(t[:, :, :, :1]), t[:, :, :, :-1]], axis=3)
        return jnp.concatenate([prev, t], axis=4)

    kk, vv = with_prev(kb), with_prev(vb)
    s = jnp.einsum('bdhnqc,bdhnkc->bdhnqk', qb, kk).astype(jnp.float32) * (hd ** -0.5)
    rel = jnp.arange(BLK)[:, None] - jnp.arange(2 * BLK)[None, :] + BLK
    k_idx = jnp.arange(n_blk)[:, None] * BLK + jnp.arange(2 * BLK)[None, :] - BLK
    valid = ((rel >= 0) & (rel <= n_keys))[None] & (k_idx >= 0)[:, None, :]
    bias = rel_bias[t5_bucket(jnp.maximum(rel, 0) * dilation)]
    bias = bias.transpose(2, 0, 1).astype(jnp.float32)[:, None]
    s = jnp.where(valid, s + bias, NEG)
    m = jnp.max(s, axis=-1, keepdims=True)
    p = jnp.exp(s - m)
    den = jnp.sum(p, axis=-1, keepdims=True)
    o = jnp.einsum('bdhnqk,bdhnkc->bdhnqc', p.astype(v.dtype), vv).astype(jnp.float32) / den
    lse = (m + jnp.log(den))[..., 0]
    o = o.reshape(bsz, dilation, heads, n_blk * BLK, hd)[:, :, :, :sub_len]
    o = o.transpose(0, 3, 1, 2, 4).reshape(bsz, seq, heads, hd)
    lse = lse.reshape(bsz, dilation, heads, n_blk * BLK)[..., :sub_len]
    lse = lse.transpose(0, 3, 1, 2).reshape(bsz, seq, heads)
    return o, lse


def dilated_mixture(q, k, v, rel_bias):
    outs, lses = [], []
    for window, dilation in DILATED_BRANCHES:
        o, l = dilated_branch(q, k, v, rel_bias, window, dilation)
        outs.append(o)
        lses.append(l)
    wts = jax.nn.softmax(jnp.stack(lses, axis=0), axis=0)
    return jnp.sum(wts[..., None] * jnp.stack(outs, axis=0), axis=0)


def setup_inputs(seed: int = 0) -> dict:
    key = jax.random.key(seed)
    ks = jax.random.split(key, 20)
    f32 = jnp.float32

    def nrm(k, shape, scale):
        return jax.random.normal(k, shape, f32) * scale

    return {
        'x': nrm(ks[0], (BATCH, SEQ, D_MODEL), 1.0),
        'norm_mix_g': 1.0 + nrm(ks[1], (DEPTH, D_MODEL), 0.02),
        'w_in': nrm(ks[2], (DEPTH, D_MODEL, IN_COLS), D_MODEL ** -0.5),
        'conv_a_w': nrm(ks[3], (DEPTH, SHORT_CONV, A_WIDTH), SHORT_CONV ** -0.5),
        'conv_c_w': nrm(ks[4], (DEPTH, CONFORMER_CONV, C_WIDTH), CONFORMER_CONV ** -0.5),
        'conv_c_b': nrm(ks[5], (DEPTH, C_WIDTH), 0.02),
        'ln_c_g': 1.0 + nrm(ks[6], (DEPTH, C_WIDTH), 0.02),
        'ln_c_b': nrm(ks[7], (DEPTH, C_WIDTH), 0.02),
        'out_norm_g': 1.0 + nrm(ks[8], (DEPTH, D_MIX), 0.02),
        'w_out': nrm(ks[9], (DEPTH, D_MIX, D_MODEL), D_MIX ** -0.5),
        'norm_ffn_g': 1.0 + nrm(ks[10], (DEPTH, D_MODEL), 0.02),
        'w_up': nrm(ks[11], (DEPTH, D_MODEL, 2 * D_FF), D_MODEL ** -0.5),
        'conv_f_w': nrm(ks[12], (DEPTH, FFN_CONV, 2 * D_FF), FFN_CONV ** -0.5),
        'w_down': nrm(ks[13], (DEPTH, D_FF, D_MODEL), D_FF ** -0.5),
        'rel_bias': nrm(ks[14], (NUM_BUCKETS, B_HEADS), 0.5),
        'final_g': 1.0 + nrm(ks[15], (D_MODEL,), 0.02),
    }


def reference(x, norm_mix_g, w_in, conv_a_w, conv_c_w, conv_c_b, ln_c_g, ln_c_b,
              out_norm_g, w_out, norm_ffn_g, w_up, conv_f_w, w_down, rel_bias, final_g):
    bsz, seq, _ = x.shape
    split_idx = list(np.cumsum(SPLIT_SIZES)[:-1])
    g_idx = [A_WIDTH, A_WIDTH + B_WIDTH]
    for l in range(DEPTH):
        h = rmsnorm(x, norm_mix_g[l])
        z = h @ w_in[l]
        a_h, a_b, a_c, q, k, v, c_val, c_gate = jnp.split(z, split_idx, axis=-1)
        y_a = a_b * causal_dwconv(a_c * a_h, conv_a_w[l])
        hs = (bsz, seq, B_HEADS, D_HEAD)
        y_b = dilated_mixture(q.reshape(hs), k.reshape(hs), v.reshape(hs), rel_bias)
        y_b = y_b.reshape(bsz, seq, B_WIDTH).astype(x.dtype)
        u = c_val * jax.nn.sigmoid(c_gate)
        u = causal_dwconv(u, conv_c_w[l]) + conv_c_b[l].astype(u.dtype)
        y_c = jax.nn.silu(layernorm(u, ln_c_g[l], ln_c_b[l]))
        g_a, g_b, g_c = jnp.split(out_norm_g[l], g_idx)
        y = jnp.concatenate([rmsnorm(y_a, g_a), rmsnorm(y_b, g_b), rmsnorm(y_c, g_c)], axis=-1)
        x = x + y @ w_out[l]
        h = rmsnorm(x, norm_ffn_g[l])
        up = causal_dwconv(h @ w_up[l], conv_f_w[l])
        gate, val = jnp.split(up, 2, axis=-1)
        x = x + (jax.nn.silu(gate) * val) @ w_down[l]
    return rmsnorm(x, final_g)
```

```python
import math
from contextlib import ExitStack

import numpy as np
import concourse.bass as bass
import concourse.mybir as mybir
from concourse.bass_utils import run_bass_kernel_spmd

F32 = mybir.dt.float32
BF16 = mybir.dt.bfloat16
ALU = mybir.AluOpType
AF = mybir.ActivationFunctionType

S = 2048
D = 1024
DEPTH = 4
IN_COLS = 2816
DFF = 2816
EPS = 1e-6
NCORES = 8

PO_GMIX = 0
PO_GFFN = 32
PO_GOUT = 64
PO_GFIN = 96
PO_CVA = 104
PO_CVC = 128
PO_CVCB = 376
PO_LNG = 384
PO_LNB = 392
PO_HLNB = 400
PO_CVF = 408
NPRM = 408 + 528


class _Op:
    __slots__ = ("eng", "fn", "reads", "writes", "dma", "waits", "sig", "need_sig", "deps")


class Prog:
    ENGS = ("pe", "act", "dve", "pool", "sp")

    def __init__(self):
        self.ops = []
        self.enabled = True

    def add(self, eng, fn, r=(), w=(), dma=False):
        if not self.enabled:
            return None
        o = _Op()
        o.eng, o.fn, o.reads, o.writes, o.dma = eng, fn, tuple(r), tuple(w), dma
        o.waits, o.sig, o.need_sig, o.deps = [], None, False, ()
        self.ops.append(o)
        return o

    def resolve(self):
        ops = self.ops
        last_w = {}
        rd_eng = {}
        rd_dma = {}
        for i, o in enumerate(ops):
            deps = set()
            for k in o.reads:
                j = last_w.get(k)
                if j is not None:
                    deps.add(j)
            for k in o.writes:
                j = last_w.get(k)
                if j is not None:
                    deps.add(j)
                d = rd_eng.get(k)
                if d:
                    deps.update(d.values())
                d2 = rd_dma.get(k)
                if d2:
                    deps.update(d2)
            deps.discard(i)
            o.deps = tuple(j for j in deps if ops[j].dma or ops[j].eng != o.eng)
            for j in o.deps:
                ops[j].need_sig = True
            for k in o.writes:
                last_w[k] = i
                rd_eng.pop(k, None)
                rd_dma.pop(k, None)
            wset = set(o.writes)
            for k in o.reads:
                if k in wset:
                    continue
                if o.dma:
                    rd_dma.setdefault(k, []).append(i)
                else:
                    rd_eng.setdefault(k, {})[o.eng] = i

    def emit(self, nc, stack, ndma=12):
        self.resolve()
        ops = self.ops
        sems = {e: stack.enter_context(nc.semaphore("sem_" + e)) for e in self.ENGS}
        dsems = {q: [stack.enter_context(nc.semaphore("dsem_%s_%d" % (q, i))) for i in range(ndma)]
                 for q in ("sp", "pool")}
        cnt = {e: 0 for e in self.ENGS}
        dcnt = {"sp": 0, "pool": 0}
        pre = {}
        for i, o in enumerate(ops):
            if o.dma:
                k = dcnt[o.eng]
                dcnt[o.eng] += 1
                sem = dsems[o.eng][k % ndma]
                o.sig = (sem, 16 * (k // ndma + 1))
                if k >= ndma:
                    pre[i] = (sem, 16 * (k // ndma))
            elif o.need_sig:
                cnt[o.eng] += 1
                o.sig = (sems[o.eng], cnt[o.eng])
        waited = {e: {} for e in self.ENGS}
        for i, o in enumerate(ops):
            need = {}
            if i in pre:
                s, v = pre[i]
                need[id(s)] = (s, v)
            for j in o.deps:
                s, v = ops[j].sig
                cur = need.get(id(s))
                if cur is None or cur[1] < v:
                    need[id(s)] = (s, v)
            wl = []
            wd = waited[o.eng]
            for sid, (s, v) in need.items():
                if wd.get(sid, 0) >= v:
                    continue
                wd[sid] = v
                wl.append((s, v))
            o.waits = wl
        by_eng = {e: [o for o in ops if o.eng == e] for e in self.ENGS}

        def run(handle, name):
            for o in by_eng[name]:
                for s, v in o.waits:
                    handle.wait_ge(s, v)
                inst = o.fn(handle)
                if o.sig is not None:
                    assert inst is not None
                    inst.then_inc(o.sig[0], 16 if o.dma else 1)

        with nc.Block() as block:
            @block.tensor
            def _(e):
                run(e, "pe")

            @block.scalar
            def _(e):
                run(e, "act")

            @block.vector
            def _(e):
                run(e, "dve")

            @block.gpsimd
            def _(e):
                run(e, "pool")

            @block.sync
            def _(e):
                run(e, "sp")


class Rot:
    def __init__(self, items):
        self.items = list(items)
        self.i = 0

    def next(self):
        v = self.items[self.i % len(self.items)]
        self.i += 1
        return v


def attn_items(hf):
    items = []
    for bi, d in enumerate((1, 4, 16)):
        L = S // d
        Lh = 1024 // d
        qlo, qhi = Lh * hf, Lh * (hf + 1)
        for r in range(d):
            for kt in range(L // 128):
                a = max(128 * kt, qlo)
                b = min(128 * kt + 256, qhi, L)
                if b <= a:
                    continue
                kp = min(128, b - 128 * kt)
                items.append((bi, d, r, 128 * kt, kp, a, b - a, a - 128 * kt))
    return items


E_OFF = (0, 256, 512)
E_W = 640


def build_program(n_layers, final_norm=True, phases="NACQTBF", dump=False):
    nc = bass.Bass("TRN2", target_bir_lowering=False)
    P = Prog()
    stack = ExitStack()

    x_d = nc.dram_tensor("x", [S, D], F32, kind="ExternalInput").ap()
    w_in_d = nc.dram_tensor("w_in", [DEPTH, D, IN_COLS], F32, kind="ExternalInput").ap()
    w_out_d = nc.dram_tensor("w_out", [DEPTH, D, D], F32, kind="ExternalInput").ap()
    w_up_d = nc.dram_tensor("w_up", [DEPTH, D, 2 * DFF], F32, kind="ExternalInput").ap()
    w_dn_d = nc.dram_tensor("w_down", [DEPTH, DFF, D], F32, kind="ExternalInput").ap()
    prm_d = nc.dram_tensor("prm", [128, NPRM], F32, kind="ExternalInput").ap()
    bg_d = nc.dram_tensor("biasg", [128, 8 * E_W], F32, kind="ExternalInput").ap()
    mk_d = nc.dram_tensor("mask01", [128, E_W], F32, kind="ExternalInput").ap()
    out_d = nc.dram_tensor("out", [S, D], F32, kind="ExternalOutput").ap()
    vscr_d = nc.dram_tensor("vscr", [S, 512], BF16, kind="Internal").ap()

    sb = lambda name, shape, dt: stack.enter_context(nc.sbuf_tensor(name, shape, dt))
    xT_t = sb("xT", [128, 8 * S], F32)
    hT_t = sb("hT", [128, 8 * S], BF16)
    qkv_t = sb("qkv", [128, 12 * S], BF16)
    yac_t = sb("yac", [128, 2 * S], BF16)
    E_t = sb("E", [128, 8 * E_W], BF16)
    WSLOT = 2048
    NW = 5
    wr_t = sb("wring", [128, NW * WSLOT], BF16)
    NSM = 6
    sm_t = sb("sm", [128, NSM * 512], F32)
    NSQ = 3
    sq_t = sb("sq", [128, NSQ * 512], BF16)
    prm_t = sb("prm_sb", [128, NPRM], F32)
    identF_t = sb("identF", [128, 128], F32)
    identB_t = sb("identB", [128, 128], BF16)
    onesB_t = sb("onesB", [128, 128], BF16)
    onesF_t = sb("onesF", [128, 64], F32)
    NDG = 6
    dg_t = sb("diag", [128, NDG * 128], BF16)
    halo_t = sb("halo", [128, 44 * 2], F32)
    ps_t = stack.enter_context(nc.psum_tensor("ps", [128, 8 * 512], F32))

    xT = xT_t[:].rearrange("p (c t) -> p c t", c=8)
    hT = hT_t[:].rearrange("p (c t) -> p c t", c=8)
    q4 = qkv_t[:, 0:4 * S].rearrange("p (c t) -> p c t", c=4)
    k4 = qkv_t[:, 4 * S:8 * S].rearrange("p (c t) -> p c t", c=4)
    v4 = qkv_t[:, 8 * S:12 * S].rearrange("p (c t) -> p c t", c=4)
    yac = yac_t[:].rearrange("p (c t) -> p c t", c=2)
    pbuf_t = qkv_t[:, 4 * S:4 * S + 2 * (2 + S)].bitcast(F32)
    E3 = E_t[:].rearrange("p (h w) -> p h w", h=8)
    prm = prm_t
    bank = lambda i: ps_t[:, 512 * i:512 * (i + 1)]

    def pcol(off):
        return prm[:, off:off + 1]

    GB = 1024

    def KR(region, b0, nb):
        return [(region, g) for g in range(b0 // GB, (b0 + nb - 1) // GB + 1)]

    def kx(c, n): return ("x", c, n)
    def kh(c, n): return ("H", c * 4 + n)
    KH_ALL = KR("H", 0, 32768)
    QOFF = {"q": 0, "k": 16384, "v": 32768}
    def kqkv(which, p, n): return ("Q", (QOFF[which] + (p * 2048 + n * 512) * 2) // GB)
    def kqkv_rng(which, p, t0, t1): return KR("Q", QOFF[which] + (p * 2048 + t0) * 2, (t1 - t0) * 2)
    def kb(i): return ("ps", i)
    def kyac(c, n): return ("Y", c * 4 + n)

    def kyac_all(): return KR("Y", 0, 8192)
    vt_t = sb("vt", [128, 4 * 260], BF16)

    P.add("sp", lambda e: e.dma_start(out=prm[:, :], in_=prm_d[:, :]), w=["prm"], dma=True)

    def f_ident(e):
        e.memset(identF_t[:], 0.0)
        return e.affine_select(out=identF_t[:], in_=identF_t[:], pattern=[[-1, 128]], compare_op=ALU.not_equal,
                               fill=1.0, base=0, channel_multiplier=1)
    P.add("pool", f_ident, w=["identF"])
    P.add("dve", lambda e: e.tensor_copy(out=identB_t[:], in_=identF_t[:]), r=["identF"], w=["identB"])
    P.add("dve", lambda e: e.memset(onesB_t[:], 1.0), w=["onesB"])
    P.add("dve", lambda e: e.memset(onesF_t[:], 1.0), w=["onesF"])
    P.add("dve", lambda e: e.memset(vt_t[:], 1.0), w=[("vt", i) for i in range(4)])
    P.add("dve", lambda e: e.tensor_scalar(out=prm[:, PO_HLNB:PO_HLNB + 8], in0=prm[:, PO_LNB:PO_LNB + 8],
                                           scalar1=0.5, scalar2=None, op0=ALU.mult), r=["prm"], w=["prm"])
    stgF = hT_t[:].bitcast(F32)
    stgQ = qkv_t[:].bitcast(F32)
    stg3 = stgF.rearrange("p (j d) -> p j d", j=8)
    stg3b = stgQ[:, 0:8192].rearrange("p (j d) -> p j d", j=8)
    P.add("sp", lambda e: e.dma_start(out=stg3, in_=x_d[0:1024, :].rearrange("(j p) d -> p j d", p=128)),
          w=KH_ALL, dma=True)
    bgv = stgQ[:, 0:8 * E_W]
    mkv = stgQ[:, 8 * E_W:9 * E_W]
    K_BG = KR("Q", 0, 8 * E_W * 4)
    K_MK = KR("Q", 8 * E_W * 4, E_W * 4)
    P.add("sp", lambda e: e.dma_start(out=bgv, in_=bg_d[:, :]), w=K_BG, dma=True)
    P.add("sp", lambda e: e.dma_start(out=mkv, in_=mk_d[:, :]), w=K_MK, dma=True)
    P.add("act", lambda e: e.activation(out=bgv, in_=bgv, func=AF.Exp), r=K_BG, w=K_BG)

    def f_E(e):
        ins = None
        bg3 = bgv.rearrange("p (h w) -> p h w", h=8)
        for h in range(8):
            ins = e.tensor_tensor(out=E3[:, h, :], in0=bg3[:, h, :], in1=mkv, op=ALU.mult)
        return ins
    P.add("dve", f_E, r=K_BG + K_MK, w=["E"])

    brot = Rot(range(8))
    evrot = Rot(["act", "dve"])
    for half in range(2):
        stg_h = stg3 if half == 0 else stg3b
        reg_h = "H" if half == 0 else "Q"
        if half == 1:
            src = x_d[1024:2048, :].rearrange("(j p) d -> p j d", p=128)
            P.add("sp", lambda e, src=src: e.dma_start(out=stg3b, in_=src), w=KR("Q", 0, 32768), dma=True)
        for c in range(8):
            for g in range(2):
                bi = brot.next()

                def f_tr(e, c=c, g=g, bi=bi, stg_h=stg_h):
                    ins = None
                    for i in range(4):
                        ins = e.transpose(bank(bi)[:, 128 * i:128 * (i + 1)], stg_h[:, g * 4 + i, c * 128:(c + 1) * 128],
                                          identF_t[:])
                    return ins
                P.add("pe", f_tr, r=KR(reg_h, g * 4 * 4096, 4 * 4096) + ["identF"], w=[kb(bi)])
                n = half * 2 + g
                dst = xT[:, c, n * 512:(n + 1) * 512]
                if evrot.next() == "act":
                    P.add("act", lambda e, dst=dst, bi=bi: e.copy(out=dst, in_=bank(bi)), r=[kb(bi)], w=[kx(c, n)])
                else:
                    P.add("dve", lambda e, dst=dst, bi=bi: e.tensor_copy(out=dst, in_=bank(bi)), r=[kb(bi)], w=[kx(c, n)])

    smrot = Rot(range(NSM))
    sqrot = Rot(range(NSQ))
    wrot = Rot(range(NW))
    dgrot = Rot(range(NDG))

    def sm(i): return sm_t[:, 512 * i:512 * (i + 1)]
    def ksm(i): return ("sm", i)
    def sq(i): return sq_t[:, 512 * i:512 * (i + 1)]
    def ksq(i): return ("sq", i)
    def wslot(i): return wr_t[:, WSLOT * i:WSLOT * (i + 1)]
    def kw(i): return ("w", i)

    def load_w(src_ap, shape_kc, ncols):
        si = wrot.next()
        assert shape_kc * ncols <= WSLOT
        dst = wslot(si)[:, 0:shape_kc * ncols].rearrange("p (k n) -> p k n", k=shape_kc)
        P.add("pool", lambda e: e.dma_start(out=dst, in_=src_ap.rearrange("(k p) n -> p k n", p=128)),
              w=[kw(si)], dma=True)
        return si, dst

    def rstd_from_ss(bi, scale):
        si = smrot.next()

        def f(e):
            e.activation(out=sm(si), in_=bank(bi), func=AF.Ln, bias=EPS, scale=scale)
            return e.activation(out=sm(si), in_=sm(si), func=AF.Exp, scale=-0.5)
        P.add("act", f, r=[kb(bi)], w=[ksm(si)])
        return si

    def rmsnorm_tiles(gcol, token_tiles, dst_of):
        for n in token_tiles:
            bi = brot.next()
            for c in range(8):
                qi = sqrot.next()
                P.add("act", lambda e, c=c, n=n, qi=qi: e.activation(out=sq(qi), in_=xT[:, c, n * 512:(n + 1) * 512],
                                                                     func=AF.Square),
                      r=[kx(c, n)], w=[ksq(qi)])
                P.add("pe", lambda e, c=c, qi=qi, bi=bi: e.matmul(bank(bi), lhsT=onesB_t[:], rhs=sq(qi),
                                                                  start=(c == 0), stop=(c == 7)),
                      r=[ksq(qi), "onesB"], w=[kb(bi)])
            si = rstd_from_ss(bi, 1.0 / D)
            for c in range(8):
                dap, dkeys = dst_of(c, n)
                P.add("dve", lambda e, c=c, n=n, si=si, dap=dap: e.scalar_tensor_tensor(
                    out=dap, in0=xT[:, c, n * 512:(n + 1) * 512], scalar=pcol(gcol + c), in1=sm(si),
                    op0=ALU.mult, op1=ALU.mult), r=[kx(c, n), ksm(si), "prm"], w=dkeys)

    def hrhs(kc, n):
        return hT[:, kc, n * 512:(n + 1) * 512]

    def mm8(wv, col0, rhs_fn, n, bi):
        def f(e):
            ins = None
            for kc in range(8):
                ins = e.matmul(bank(bi), lhsT=wv[:, kc, col0:col0 + 128], rhs=rhs_fn(kc, n),
                               start=(kc == 0), stop=(kc == 7))
            return ins
        return f

    for l in range(n_layers):
        P.enabled = "N" in phases
        rmsnorm_tiles(PO_GMIX + 8 * l, range(4), lambda c, n: (hT[:, c, n * 512:(n + 1) * 512], [kh(c, n)]))
        KHN = lambda n: [kh(kc, n) for kc in range(8)]

        P.enabled = "A" in phases
        sA = {}
        for name, col in (("h", 0), ("b", 256), ("c", 512)):
            sA[name] = load_w(w_in_d[l, :, col:col + 256], 8, 256)
        brot = Rot(range(8))
        pendA = []
        def kpb(t0, nt): return KR("Q", 16384 + t0 * 4, nt * 4)
        P.add("dve", lambda e: e.memset(pbuf_t[:, 0:2], 0.0), w=kpb(0, 2))
        for cc in range(2):
            for n in range(4):
                si_w, wv = sA["h"]
                bi = brot.next()
                P.add("pe", mm8(wv, cc * 128, hrhs, n, bi), r=KHN(n) + [kw(si_w)], w=[kb(bi)])
                si = smrot.next()
                P.add("act", lambda e, si=si, bi=bi: e.copy(out=sm(si), in_=bank(bi)), r=[kb(bi)], w=[ksm(si)])
                si_c, wc = sA["c"]
                bi2 = brot.next()
                P.add("pe", mm8(wc, cc * 128, hrhs, n, bi2), r=KHN(n) + [kw(si_c)], w=[kb(bi2)])
                P.add("dve", lambda e, n=n, bi2=bi2, si=si: e.tensor_tensor(
                    out=pbuf_t[:, 2 + 512 * n:2 + 512 * (n + 1)], in0=bank(bi2), in1=sm(si), op=ALU.mult),
                    r=[kb(bi2), ksm(si)], w=kpb(2 + 512 * n, 512))
            si_b, wb = sA["b"]
            for n in range(4):
                bi = brot.next()
                P.add("pe", mm8(wb, cc * 128, hrhs, n, bi), r=KHN(n) + [kw(si_b)], w=[kb(bi)])
                si = smrot.next()
                cw = PO_CVA + (l * 3) * 2 + cc
                pk = kpb(512 * n, 514)
                P.add("act", lambda e, n=n, si=si, cw=cw: e.activation(
                    out=sm(si), in_=pbuf_t[:, 2 + 512 * n:2 + 512 * (n + 1)], func=AF.Identity, scale=pcol(cw + 4)),
                    r=pk + ["prm"], w=[ksm(si)])

                def f4(e, n=n, si=si, cw=cw):
                    e.scalar_tensor_tensor(out=sm(si), in0=pbuf_t[:, 1 + 512 * n:1 + 512 * (n + 1)], scalar=pcol(cw + 2),
                                           in1=sm(si), op0=ALU.mult, op1=ALU.add)
                    return e.scalar_tensor_tensor(out=sm(si), in0=pbuf_t[:, 512 * n:512 * (n + 1)], scalar=pcol(cw),
                                                  in1=sm(si), op0=ALU.mult, op1=ALU.add)
                P.add("dve", f4, r=pk + [ksm(si), "prm"], w=[ksm(si)])
                P.add("dve", lambda e, n=n, si=si, bi=bi, cc=cc: e.tensor_tensor(
                    out=yac[:, cc, 512 * n:512 * (n + 1)], in0=bank(bi), in1=sm(si), op=ALU.mult),
                    r=[kb(bi), ksm(si)], w=[kyac(cc, n)])
                pendA.append((cc, n))

        def finish_group(ss_banks, scale_ss, row0, post_scale, mid=None):
            rs = {}
            for n in range(4):
                rs[n] = rstd_from_ss(ss_banks[n], scale_ss)
            for n in range(4):
                for cc in range(2):
                    P.add("dve", lambda e, n=n, cc=cc: e.scalar_tensor_tensor(
                        out=yac[:, cc, 512 * n:512 * (n + 1)], in0=yac[:, cc, 512 * n:512 * (n + 1)], scalar=post_scale,
                        in1=sm(rs[n]), op0=ALU.mult, op1=ALU.mult), r=[kyac(cc, n), ksm(rs[n])], w=[kyac(cc, n)])
            si_w, wv = load_w(w_out_d[l, row0:row0 + 256, :], 2, 1024)
            for kc in range(2):
                gc = PO_GOUT + 8 * l + row0 // 128 + kc
                P.add("dve", lambda e, kc=kc, gc=gc, wv=wv: e.tensor_scalar(
                    out=wv[:, kc, :], in0=wv[:, kc, :], scalar1=pcol(gc), scalar2=None, op0=ALU.mult),
                    r=[kw(si_w), "prm"], w=[kw(si_w)])
            if mid is not None:
                en = P.enabled
                mid()
                P.enabled = en
            for m in range(8):
                for n in range(4):
                    bi = brot.next()

                    def f(e, m=m, n=n, bi=bi, wv=wv):
                        e.matmul(bank(bi), lhsT=wv[:, 0, m * 128:(m + 1) * 128], rhs=yac[:, 0, 512 * n:512 * (n + 1)],
                                 start=True, stop=False)
                        return e.matmul(bank(bi), lhsT=wv[:, 1, m * 128:(m + 1) * 128],
                                        rhs=yac[:, 1, 512 * n:512 * (n + 1)], start=False, stop=True)
                    P.add("pe", f, r=[kyac(0, n), kyac(1, n), kw(si_w)], w=[kb(bi)])
                    P.add("dve", lambda e, m=m, n=n, bi=bi: e.tensor_tensor(
                        out=xT[:, m, 512 * n:512 * (n + 1)], in0=xT[:, m, 512 * n:512 * (n + 1)], in1=bank(bi),
                        op=ALU.add), r=[kb(bi), kx(m, n)], w=[kx(m, n)])

        UW = 30 + S
        u3 = qkv_t[:, 0:2 * UW].rearrange("p (c t) -> p c t", c=2)
        def ku(cc, t0, nt): return KR("Q", (cc * UW + t0) * 2, nt * 2)

        def c_head():
            P.enabled = "C" in phases
            for cc in range(2):
                P.add("dve", lambda e, cc=cc: e.memset(u3[:, cc, 0:30], 0.0), w=ku(cc, 0, 30))
            sC = {}
            for name, col in (("val", 2304), ("gate", 2560)):
                sC[name] = load_w(w_in_d[l, :, col:col + 256], 8, 256)
            for cc in range(2):
                for n in range(4):
                    si_g, wg = sC["gate"]
                    si_v, wvv = sC["val"]
                    bg_, bv_ = brot.next(), brot.next()
                    P.add("pe", mm8(wg, cc * 128, hrhs, n, bg_), r=KHN(n) + [kw(si_g)], w=[kb(bg_)])
                    P.add("pe", mm8(wvv, cc * 128, hrhs, n, bv_), r=KHN(n) + [kw(si_v)], w=[kb(bv_)])
                    si = smrot.next()
                    P.add("act", lambda e, si=si, b=bg_: e.activation(out=sm(si), in_=bank(b), func=AF.Tanh, scale=0.5),
                          r=[kb(bg_)], w=[ksm(si)])
                    P.add("dve", lambda e, si=si, b=bv_, n=n, cc=cc: e.scalar_tensor_tensor(
                        out=u3[:, cc, 30 + 512 * n:30 + 512 * (n + 1)], in0=sm(si), scalar=1.0, in1=bank(b),
                        op0=ALU.add, op1=ALU.mult), r=[ksm(si), kb(bv_)], w=ku(cc, 30 + 512 * n, 512))

        P.enabled = "A" in phases
        ssA = [brot.next() for _ in range(4)]
        for n in range(4):
            for cc in range(2):
                qi = sqrot.next()
                P.add("act", lambda e, n=n, qi=qi, cc=cc: e.activation(out=sq(qi), in_=yac[:, cc, 512 * n:512 * (n + 1)],
                                                                       func=AF.Square), r=[kyac(cc, n)], w=[ksq(qi)])
                P.add("pe", lambda e, bk=ssA[n], qi=qi, cc=cc: e.matmul(bank(bk), lhsT=onesB_t[:], rhs=sq(qi),
                                                                      start=(cc == 0), stop=(cc == 1)),
                      r=[ksq(qi), "onesB"], w=[kb(ssA[n])])
        finish_group(ssA, 1.0 / 256, 0, 1.0, mid=c_head)
        NPT = 6
        PTW = 1024
        pt_v = hT_t[:, 0:NPT * PTW]
        OSB0 = NPT * PTW * 2
        osb_v = hT_t[:, NPT * PTW:NPT * PTW + 6 * 1024].bitcast(F32)
        ptrot = Rot(range(NPT))
        vtrot = Rot(range(4))

        def pt4(i): return pt_v[:, PTW * i:PTW * (i + 1)].rearrange("p (h i q) -> p h i q", h=2, i=2)
        def kpt(i, hh): return KR("H", PTW * 2 * i + 1024 * hh, 1024)
        def vt4(i): return vt_t[:, 260 * i:260 * (i + 1)].rearrange("p (i h d) -> p i h d", i=2, h=2)
        def osb(hh): return osb_v[:, 1024 * hh:1024 * (hh + 1)]
        def kosb(hh, j): return KR("H", OSB0 + hh * 4096 + j * 2048, 2048)
        rbc_v = osb_v[:, 2048:3072]
        K_RD = KR("H", OSB0 + 8192, 4096)
        VPW = 16 * 2 * 128
        VP2_EL = NPT * PTW + 6 * 1024
        assert VP2_EL + VPW <= 8 * S

        def vp_flat(b_):
            if b_ < 2:
                return qkv_t[:, 8 * S + VPW * b_:8 * S + VPW * (b_ + 1)]
            return hT_t[:, VP2_EL:VP2_EL + VPW]
        def vp4(b_): return vp_flat(b_).rearrange("p (t h d) -> p t h d", t=16, h=2)
        def kvp(b_): return KR("Q", 32768 + VPW * 2 * b_, VPW * 2) if b_ < 2 else KR("H", VP2_EL * 2, VPW * 2)
        VP_ALL = [("vp", b_, g_, h_) for b_ in range(3) for g_ in range(4) for h_ in range(2)]
        VP_01 = [k_ for k_ in VP_ALL if k_[1] < 2]
        VP_2 = [k_ for k_ in VP_ALL if k_[1] == 2]
        VSCR_ALL = [("vscr", tt) for tt in range(16)]

        def load_vp(p, b_):
            cols = slice(128 * p, 128 * (p + 1))
            if b_ == 0:
                src = vscr_d[:, cols].rearrange("(t q) c -> q t c", q=128)
            elif b_ == 1:
                src = vscr_d[:, cols].rearrange("(n q r) c -> q r n c", q=128, r=4)
            else:
                src = vscr_d[:, cols].rearrange("(q r) c -> q r c", r=16)
            for t0 in range(0, 16, 4):
                if b_ == 1:
                    sap = src[:, t0 // 4, :, :]
                else:
                    sap = src[:, t0:t0 + 4, :]
                for hh in range(2):
                    dap = vp4(b_)[:, t0:t0 + 4, hh, 0:64]
                    P.add("sp", lambda e, sap=sap, dap=dap, hh=hh: e.dma_start(out=dap, in_=sap[:, :, 64 * hh:64 * (hh + 1)]),
                          r=VSCR_ALL, w=[("vp", b_, t0 // 4, hh)], dma=True)
        P.enabled = "Q" in phases
        vsl = [load_w(w_in_d[l, :, 1792 + 256 * sl_:1792 + 256 * (sl_ + 1)], 8, 256) for sl_ in range(2)]
        for tt in range(16):
            bi = brot.next()

            def fv(e, tt=tt, bi=bi, vsl=vsl):
                ins = None
                for sl_ in range(2):
                    wv = vsl[sl_][1]
                    for kc in range(8):
                        ins = e.matmul(bank(bi)[:, 256 * sl_:256 * (sl_ + 1)], lhsT=hT[:, kc, 128 * tt:128 * (tt + 1)],
                                       rhs=wv[:, kc, :], start=(kc == 0), stop=(kc == 7), skip_group_check=True)
                return ins
            P.add("pe", fv, r=KHN(tt // 4) + [kw(vsl[0][0]), kw(vsl[1][0])], w=[kb(bi)])
            qi = sqrot.next()
            if evrot.next() == "act":
                P.add("act", lambda e, qi=qi, bi=bi: e.copy(out=sq(qi), in_=bank(bi)), r=[kb(bi)], w=[ksq(qi)])
            else:
                P.add("dve", lambda e, qi=qi, bi=bi: e.tensor_copy(out=sq(qi), in_=bank(bi)), r=[kb(bi)], w=[ksq(qi)])
            P.add("sp", lambda e, qi=qi, tt=tt: e.dma_start(out=vscr_d[128 * tt:128 * (tt + 1), :], in_=sq(qi)),
                  r=[ksq(qi)], w=[("vscr", tt)], dma=True)
        P.add("dve", lambda e: e.memset(qkv_t[:, 8 * S:8 * S + 2 * VPW], 1.0), w=kvp(0) + kvp(1) + VP_01)
        load_vp(0, 0)
        load_vp(0, 1)
        P.enabled = "C" in phases
        brot = Rot(range(4))
        convb = [4, 5, 6, 7]
        for cc in range(2):
            for k in range(31):
                di = dgrot.next()
                wc = PO_CVC + (l * 31 + k) * 2 + cc
                P.add("dve", lambda e, di=di, wc=wc: e.tensor_scalar(
                    out=dg_t[:, 128 * di:128 * (di + 1)], in0=identB_t[:], scalar1=pcol(wc), scalar2=0.5,
                    op0=ALU.mult, op1=ALU.mult), r=["identB", "prm"], w=[("dg", di)])
                for n in range(4):
                    P.add("pe", lambda e, di=di, n=n, k=k, cc=cc: e.matmul(
                        bank(convb[n]), lhsT=dg_t[:, 128 * di:128 * (di + 1)], rhs=u3[:, cc, 512 * n + k:512 * n + k + 512],
                        start=(k == 0), stop=(k == 30)), r=ku(cc, 512 * n + k, 512) + [("dg", di)], w=[kb(convb[n])])
            for n in range(4):
                bc_ = PO_CVCB + 2 * l + cc
                P.add("act", lambda e, n=n, cc=cc, bc_=bc_: e.activation(
                    out=yac[:, cc, 512 * n:512 * (n + 1)], in_=bank(convb[n]), func=AF.Identity, bias=pcol(bc_), scale=1.0),
                    r=[kb(convb[n]), "prm"], w=[kyac(cc, n)])
        def qkv_units():
            for which, col0, dst4, scl in (("q", 768, q4, 0.125), ("k", 1280, k4, 1.0)):
                for sl in range(2):
                    si_w, wv = load_w(w_in_d[l, :, col0 + 256 * sl:col0 + 256 * (sl + 1)], 8, 256)
                    for j in range(2):
                        p = sl * 2 + j
                        for n in range(4):
                            bi = brot.next()
                            P.add("pe", mm8(wv, j * 128, hrhs, n, bi), r=KHN(n) + [kw(si_w)], w=[kb(bi)])
                            dst = dst4[:, p, 512 * n:512 * (n + 1)]
                            if evrot.next() == "act":
                                P.add("act", lambda e, dst=dst, bi=bi, scl=scl: e.activation(out=dst, in_=bank(bi), func=AF.Copy,
                                                                                             scale=scl),
                                      r=[kb(bi)], w=[kqkv(which, p, n)])
                            else:
                                P.add("dve", lambda e, dst=dst, bi=bi, scl=scl: e.tensor_scalar(
                                    out=dst, in0=bank(bi), scalar1=scl, scalar2=None, op0=ALU.mult),
                                    r=[kb(bi)], w=[kqkv(which, p, n)])
                            yield
        qkv_gen = qkv_units()

        def qkv_emit(k):
            en = P.enabled
            P.enabled = "Q" in phases
            for _ in range(k):
                next(qkv_gen, None)
            P.enabled = en

        ssC = [4, 5, 6, 7]
        pend_ss = []
        for n in range(4):
            b1, b2 = brot.next(), brot.next()
            for cc in range(2):
                P.add("pe", lambda e, n=n, cc=cc, b1=b1: e.matmul(bank(b1), lhsT=onesB_t[:], rhs=yac[:, cc, 512 * n:512 * (n + 1)],
                                                                  start=(cc == 0), stop=(cc == 1)),
                      r=[kyac(cc, n), "onesB"], w=[kb(b1)])
                qi = sqrot.next()
                P.add("act", lambda e, n=n, cc=cc, qi=qi: e.activation(out=sq(qi), in_=yac[:, cc, 512 * n:512 * (n + 1)],
                                                                       func=AF.Square), r=[kyac(cc, n)], w=[ksq(qi)])
                P.add("pe", lambda e, cc=cc, b2=b2, qi=qi: e.matmul(bank(b2), lhsT=onesB_t[:], rhs=sq(qi),
                                                                    start=(cc == 0), stop=(cc == 1)),
                      r=[ksq(qi), "onesB"], w=[kb(b2)])
            s_m, s_v = smrot.next(), smrot.next()
            P.add("dve", lambda e, s_m=s_m, b1=b1: e.tensor_scalar(out=sm(s_m), in0=bank(b1), scalar1=1.0 / 256,
                                                                   scalar2=None, op0=ALU.mult),
                  r=[kb(b1)], w=[ksm(s_m)])
            P.add("dve", lambda e, s_m=s_m, s_v=s_v: e.tensor_tensor(out=sm(s_v), in0=sm(s_m), in1=sm(s_m), op=ALU.mult),
                  r=[ksm(s_m)], w=[ksm(s_v)])
            P.add("dve", lambda e, s_v=s_v, b2=b2: e.scalar_tensor_tensor(
                out=sm(s_v), in0=bank(b2), scalar=1.0 / 256, in1=sm(s_v), op0=ALU.mult, op1=ALU.subtract),
                r=[kb(b2), ksm(s_v)], w=[ksm(s_v)])
            def frs(e, s_v=s_v):
                e.activation(out=sm(s_v), in_=sm(s_v), func=AF.Ln, bias=EPS, scale=1.0)
                return e.activation(out=sm(s_v), in_=sm(s_v), func=AF.Exp, scale=-0.5)
            P.add("act", frs, r=[ksm(s_v)], w=[ksm(s_v)])
            for cc in range(2):
                s_d, s_t = smrot.next(), smrot.next()
                gcol = PO_LNG + 2 * l + cc
                bcol = PO_LNB + 2 * l + cc
                hbcol = PO_HLNB + 2 * l + cc

                def fz(e, n=n, cc=cc, s_d=s_d, s_m=s_m, s_v=s_v, gcol=gcol):
                    e.tensor_tensor(out=sm(s_d), in0=yac[:, cc, 512 * n:512 * (n + 1)], in1=sm(s_m), op=ALU.subtract)
                    return e.scalar_tensor_tensor(out=sm(s_d), in0=sm(s_d), scalar=pcol(gcol), in1=sm(s_v),
                                                  op0=ALU.mult, op1=ALU.mult)
                P.add("dve", fz, r=[kyac(cc, n), ksm(s_m), ksm(s_v), "prm"], w=[ksm(s_d)])
                P.add("act", lambda e, n=n, cc=cc, s_d=s_d, bcol=bcol: e.activation(
                    out=yac[:, cc, 512 * n:512 * (n + 1)], in_=sm(s_d), func=AF.Silu, bias=pcol(bcol), scale=1.0),
                    r=[ksm(s_d), "prm"], w=[kyac(cc, n)])
                qi = sqrot.next()
                P.add("act", lambda e, n=n, cc=cc, qi=qi: e.activation(out=sq(qi), in_=yac[:, cc, 512 * n:512 * (n + 1)],
                                                                       func=AF.Square), r=[kyac(cc, n)], w=[ksq(qi)])
                pend_ss.append((n, cc, qi))
            qkv_emit(7)
            for (n_, cc_, qi_) in pend_ss:
                P.add("pe", lambda e, n=n_, cc=cc_, qi=qi_: e.matmul(bank(ssC[n]), lhsT=onesB_t[:], rhs=sq(qi),
                                                                     start=(cc == 0), stop=(cc == 1)),
                      r=[ksq(qi_), "onesB"], w=[kb(ssC[n_])])
            pend_ss = []
        finish_group(ssC, 1.0 / 256, 768, 1.0, mid=lambda: qkv_emit(48))
        brot = Rot(range(8))

        P.enabled = "Q" in phases
        qkv_emit(48)
        brot = Rot(range(8))

        P.enabled = "T" in phases
        P.add("dve", lambda e: e.memset(hT_t[:, VP2_EL:VP2_EL + VPW], 1.0), w=kvp(2) + VP_2)
        load_vp(0, 2)
        XB = {0: 4, 1: 5}

        deferred = []
        pq = []
        gcount = [0]

        def pq_drain(limit):
            while sum(1 for k_, _ in pq if k_ == "pv") > limit:
                pq.pop(0)[1]()
            while pq and pq[0][0] != "pv":
                pq.pop(0)[1]()
        for p in range(4):
            for hf in range(2):
                items = attn_items(hf)
                groups = []
                for it in items:
                    sig = (it[0], it[4], it[6], it[7])
                    if groups and len(groups[-1]) < 2 and groups[-1][0][1] == sig:
                        groups[-1].append((it, sig))
                    else:
                        groups.append([(it, sig)])
                started = set()
                LAG = 3

                def emit_pv(grp, pi, vi, hf=hf, started=started):
                    for ii, (it, sig) in enumerate(grp):
                        (bi_, d, r, ki0, kp, qi0, nq, qoff) = it
                        vtile = (ki0 // 128) if d == 1 else ((r * 4 + ki0 // 128) if d == 4 else r)
                        per = 512 // d
                        s0 = qi0
                        while s0 < qi0 + nq:
                            e0 = min(qi0 + nq, (s0 // per + 1) * per)
                            col = s0 * d + r - 1024 * hf
                            bsub = col // 512
                            for hh in range(2):
                                ab = 2 * hh + bsub
                                first = ab not in started
                                started.add(ab)
                                c0 = col - 512 * bsub
                                cnt = e0 - s0
                                a_, b_ = s0 - qi0, e0 - qi0
                                if d == 1:
                                    oap = bank(ab)[0:128, :].rearrange("p (r i) -> p r i", r=4)[:, :, c0 // 4:(c0 + cnt) // 4]
                                    rap = pt4(pi)[0:kp, hh, ii, a_:b_].rearrange("p (i r) -> p r i", r=4)
                                elif d == 4:
                                    il0 = (c0 - r) // 4
                                    oap = bank(ab)[0:128, r * 128 + il0:r * 128 + il0 + cnt]
                                    rap = pt4(pi)[0:kp, hh, ii, a_:b_]
                                else:
                                    r4, a4 = r % 4, r // 4
                                    il0 = (c0 - r4) // 4
                                    oap = bank(ab)[0:128, r4 * 128 + il0:r4 * 128 + il0 + (cnt - 1) * 4 + 1:4]
                                    rap = pt4(pi)[0:kp, hh, ii, a_:b_]

                                def fpv(e, oap=oap, rap=rap, bi_=bi_, vtile=vtile, hh=hh, kp=kp, first=first):
                                    return e.matmul(oap, lhsT=vp4(bi_)[0:kp, vtile, hh, 0:128], rhs=rap,
                                                    start=first, stop=False, skip_group_check=True)
                                P.add("pe", fpv, r=kpt(pi, hh) + kvp(bi_) + [("vp", bi_, vtile // 4, hh)], w=[kb(ab)])
                            s0 = e0

                for gi, grp in enumerate(groups):
                    ng = len(grp)
                    (bi_, kp, nq, qoff) = grp[0][1]
                    d = grp[0][0][1]
                    sl = []
                    for (it, sig) in grp:
                        (_, _, r, ki0, _, qi0, _, _) = it
                        kt0 = ki0 * d + r
                        kt1 = kt0 + (kp - 1) * d + 1
                        qt0 = qi0 * d + r
                        qt1 = qt0 + (nq - 1) * d + 1
                        sl.append((slice(kt0, kt1, d), slice(qt0, qt1, d), kt0, kt1, qt0, qt1))
                    vi = 0
                    pi = ptrot.next()
                    xb0 = 4 + 2 * (gcount[0] % 2)
                    gcount[0] += 1

                    def fst(e, sl=sl, kp=kp, nq=nq, p=p, xb0=xb0):
                        ins = None
                        for ii, s_ in enumerate(sl):
                            for hh in range(2):
                                ins = e.matmul(bank(xb0 + hh)[0:kp, 256 * ii:256 * ii + nq],
                                               lhsT=k4[64 * hh:64 * (hh + 1), p, s_[0]],
                                               rhs=q4[64 * hh:64 * (hh + 1), p, s_[1]], start=True, stop=True,
                                               skip_group_check=True)
                        return ins
                    kkr = [k_ for s_ in sl for k_ in kqkv_rng("k", p, s_[2], s_[3]) + kqkv_rng("q", p, s_[4], s_[5])]
                    P.add("pe", fst, r=kkr, w=[kb(xb0), kb(xb0 + 1)])
                    ec = E_OFF[bi_] + qoff
                    xin = ps_t[:, xb0 * 512:(xb0 + 2) * 512].rearrange("p (h i q) -> p h i q", h=2, i=2)[0:kp, :, 0:ng, 0:nq]
                    P.add("act", lambda e, pi=pi, kp=kp, nq=nq, ng=ng, xin=xin: e.activation(
                        out=pt4(pi)[0:kp, :, 0:ng, 0:nq], in_=xin, func=AF.Exp),
                        r=[kb(xb0), kb(xb0 + 1)], w=kpt(pi, 0) + kpt(pi, 1))
                    P.add("dve", lambda e, pi=pi, kp=kp, nq=nq, ng=ng, ec=ec, p=p: e.tensor_tensor(
                        out=pt4(pi)[0:kp, :, 0:ng, 0:nq], in0=pt4(pi)[0:kp, :, 0:ng, 0:nq],
                        in1=E3[0:kp, 2 * p:2 * p + 2, ec:ec + nq].unsqueeze(2).broadcast_to([kp, 2, ng, nq]), op=ALU.mult),
                        r=kpt(pi, 0) + kpt(pi, 1) + ["E"], w=kpt(pi, 0) + kpt(pi, 1))
                    pq.append(("pv", lambda f=emit_pv, grp=grp, pi=pi: f(grp, pi, 0)))
                    last_of_branch = (gi + 1 == len(groups)) or (groups[gi + 1][0][1][0] != bi_)
                    if hf == 1 and p + 1 < 4 and last_of_branch:
                        pq.append(("misc", lambda p=p, b_=bi_: load_vp(p + 1, b_)))
                    pq_drain(LAG)
                    if gi >= 1 and deferred:
                        deferred.pop(0)()

                def mk_post(p=p, hf=hf):
                    steps = []
                    for hh in range(2):
                        def s_rd(hh=hh):
                            def frd(e, hh=hh):
                                e.activation(out=rbc_v[0:64, :], in_=osb(hh)[64:128, :], func=AF.Ln)
                                return e.activation(out=rbc_v[0:64, :], in_=rbc_v[0:64, :], func=AF.Exp, scale=-1.0)
                            P.add("act", frd, r=kosb(hh, 0) + kosb(hh, 1), w=K_RD)
                        steps.append(s_rd)
                        for j in range(2):
                            def s_y(hh=hh, j=j, p=p, hf=hf):
                                n = 2 * hf + j
                                P.add("dve", lambda e, hh=hh, j=j, n=n, p=p: e.tensor_tensor(
                                    out=q4[64 * hh:64 * (hh + 1), p, 512 * n:512 * (n + 1)],
                                    in0=osb(hh)[0:64, 512 * j:512 * (j + 1)], in1=rbc_v[0:64, 512 * j:512 * (j + 1)], op=ALU.mult),
                                    r=kosb(hh, j) + K_RD, w=[kqkv("q", p, n)])
                            steps.append(s_y)
                    return steps
                def block_end(mk_post=mk_post):
                    while deferred:
                        deferred.pop(0)()
                    for hh in range(2):
                        P.add("dve", lambda e, hh=hh: e.tensor_copy(
                            out=osb(hh)[:, 0:512].rearrange("p (i r) -> p i r", r=4),
                            in_=bank(2 * hh)[:, :].rearrange("p (r i) -> p i r", r=4)),
                              r=[kb(2 * hh)], w=kosb(hh, 0))
                        P.add("dve", lambda e, hh=hh: e.tensor_copy(
                            out=osb(hh)[:, 512:1024].rearrange("p (i r) -> p i r", r=4),
                            in_=bank(2 * hh + 1)[:, :].rearrange("p (r i) -> p i r", r=4)),
                              r=[kb(2 * hh + 1)], w=kosb(hh, 1))
                    deferred.extend(mk_post())
                pq.append(("misc", block_end))
        pq_drain(0)
        while deferred:
            deferred.pop(0)()

        P.enabled = "B" in phases
        si_w0, wv0 = load_w(w_out_d[l, 256:512, :], 2, 1024)
        si_w1, wv1 = load_w(w_out_d[l, 512:768, :], 2, 1024)
        for kc4 in range(4):
            si_w, wv = (si_w0, wv0) if kc4 < 2 else (si_w1, wv1)
            gc = PO_GOUT + 8 * l + 2 + kc4
            P.add("dve", lambda e, kc=kc4 % 2, gc=gc, wv=wv: e.tensor_scalar(
                out=wv[:, kc, :], in0=wv[:, kc, :], scalar1=pcol(gc), scalar2=None, op0=ALU.mult),
                r=[kw(si_w), "prm"], w=[kw(si_w)])
        for n in range(4):
            bi = brot.next()
            for pp in range(4):
                qi = sqrot.next()
                P.add("act", lambda e, n=n, pp=pp, qi=qi: e.activation(out=sq(qi), in_=q4[:, pp, 512 * n:512 * (n + 1)],
                                                                       func=AF.Square),
                      r=[kqkv("q", pp, n)], w=[ksq(qi)])
                P.add("pe", lambda e, pp=pp, qi=qi, bi=bi: e.matmul(bank(bi), lhsT=onesB_t[:], rhs=sq(qi),
                                                                    start=(pp == 0), stop=(pp == 3)),
                      r=[ksq(qi), "onesB"], w=[kb(bi)])
            rs = rstd_from_ss(bi, 1.0 / 512)
            for m in range(8):
                b2 = brot.next()

                def fo(e, m=m, n=n, b2=b2, wv0=wv0, wv1=wv1):
                    ins = None
                    for pp in range(4):
                        wv = wv0 if pp < 2 else wv1
                        ins = e.matmul(bank(b2), lhsT=wv[:, pp % 2, m * 128:(m + 1) * 128],
                                       rhs=q4[:, pp, 512 * n:512 * (n + 1)], start=(pp == 0), stop=(pp == 3))
                    return ins
                P.add("pe", fo, r=[kqkv("q", pp, n) for pp in range(4)] + [kw(si_w0), kw(si_w1)], w=[kb(b2)])
                st = (rs + 1 + (m % 3)) % NSM
                P.add("dve", lambda e, b2=b2, rs=rs, st=st: e.tensor_tensor(out=sm(st), in0=bank(b2), in1=sm(rs), op=ALU.mult),
                      r=[kb(b2), ksm(rs)], w=[ksm(st)])
                P.add("dve", lambda e, m=m, n=n, st=st: e.tensor_tensor(
                    out=xT[:, m, 512 * n:512 * (n + 1)], in0=xT[:, m, 512 * n:512 * (n + 1)], in1=sm(st), op=ALU.add),
                    r=[ksm(st), kx(m, n)], w=[kx(m, n)])

        P.enabled = "F" in phases
        act3 = qkv_t[:, 0:22 * 1024].rearrange("p (j t) -> p j t", j=22)
        def kact(j, n2=None):
            return KR("Q", j * 2048, 2048) if n2 is None else KR("Q", j * 2048 + n2 * 1024, 1024)
        h2 = hT_t[:, 0:8 * 1024].rearrange("p (c t) -> p c t", c=8)
        def kh2(c, n2): return KR("H", (c * 1024 + n2 * 512) * 2, 1024)
        ft_v = hT_t[:, 8 * 1024:16 * 1024].bitcast(F32)
        ft2_v = yac_t[:].bitcast(F32)
        ftiles = [ft_v[:, 1024 * i:1024 * (i + 1)] for i in range(4)] + [ft2_v[:, 1024 * i:1024 * (i + 1)] for i in range(2)]
        def kft(i): return KR("H", 16384 + 4096 * i, 4096) if i < 4 else KR("Y", 4096 * (i - 4), 4096)

        def h2rhs(kc, n2):
            return h2[:, kc, 512 * n2:512 * (n2 + 1)]
        brot = Rot(range(8))
        for hf in range(2):
            if hf == 0:
                rmsnorm_tiles(PO_GFFN + 8 * l, [0, 1],
                              lambda c, n: (h2[:, c, 512 * (n % 2):512 * (n % 2 + 1)], kh2(c, n % 2)))
            ftrot = Rot([0, 1, 2, 3, 4, 5])
            dn = {}
            def load_dn(m, part):
                dn[(m, part)] = load_w(w_dn_d[l, 1408 * part:1408 * (part + 1), m * 128:(m + 1) * 128], 11, 128)
            upw = {}
            def load_up(jj):
                upw[jj] = (load_w(w_up_d[l, :, 256 * jj:256 * (jj + 1)], 8, 256),
                           load_w(w_up_d[l, :, DFF + 256 * jj:DFF + 256 * (jj + 1)], 8, 256))
            load_up(0)
            for jj in range(11):
                if jj + 1 < 11:
                    load_up(jj + 1)
                else:
                    load_dn(0, 0)
                    load_dn(1, 0)
                    load_dn(0, 1)
                (si_g, wg), (si_v, wv_) = upw[jj]
                for j2 in range(2):
                    j = 2 * jj + j2
                    R = {}
                    for gv, (si_w, wv) in enumerate(((si_g, wg), (si_v, wv_))):
                        ch = j + 22 * gv
                        fi = ftrot.next()
                        R[gv] = fi
                        Rt = ftiles[fi]
                        cw = PO_CVF + (l * 3) * 44 + ch
                        b0 = brot.next()
                        b1 = brot.next()
                        assert b0 % 2 == 0 and b1 == b0 + 1
                        ps2 = ps_t[:, 512 * b0:512 * (b0 + 2)]
                        for n2, bi in ((0, b0), (1, b1)):
                            P.add("pe", mm8(wv, j2 * 128, h2rhs, n2, bi),
                                  r=[k_ for kc in range(8) for k_ in kh2(kc, n2)] + [kw(si_w)], w=[kb(bi)])
                        P.add("act", lambda e, ps2=ps2, Rt=Rt, cw=cw: e.activation(
                            out=Rt[:, :], in_=ps2, func=AF.Identity, scale=pcol(cw + 88)),
                            r=[kb(b0), kb(b1), "prm"], w=kft(fi))

                        def ftap(e, Rt=Rt, ps2=ps2, cw=cw, ch=ch, hf=hf):
                            w1, w0 = pcol(cw + 44), pcol(cw)
                            stt = e.scalar_tensor_tensor
                            stt(out=Rt[:, 1:1024], in0=ps2[:, 0:1023], scalar=w1, in1=Rt[:, 1:1024], op0=ALU.mult, op1=ALU.add)
                            ins = stt(out=Rt[:, 2:1024], in0=ps2[:, 0:1022], scalar=w0, in1=Rt[:, 2:1024], op0=ALU.mult, op1=ALU.add)
                            hl = halo_t[:, 2 * ch:2 * ch + 2]
                            if hf == 1:
                                stt(out=Rt[:, 0:1], in0=hl[:, 1:2], scalar=w1, in1=Rt[:, 0:1], op0=ALU.mult, op1=ALU.add)
                                ins = stt(out=Rt[:, 0:2], in0=hl[:, 0:2], scalar=w0, in1=Rt[:, 0:2], op0=ALU.mult, op1=ALU.add)
                            else:
                                ins = e.tensor_copy(out=hl, in_=ps2[:, 1022:1024])
                            return ins
                        P.add("dve", ftap, r=[kb(b0), kb(b1), "prm", ("halo", ch)] + kft(fi), w=kft(fi) + [("halo", ch)])
                    fT = ftrot.next()
                    Tt = ftiles[fT]
                    Rg, Rv = ftiles[R[0]], ftiles[R[1]]
                    P.add("act", lambda e, Tt=Tt, Rg=Rg: e.activation(out=Tt, in_=Rg, func=AF.Silu),
                          r=kft(R[0]), w=kft(fT))
                    P.add("pool", lambda e, Tt=Tt, Rv=Rv, j=j: e.tensor_tensor(out=act3[:, j, :], in0=Tt, in1=Rv, op=ALU.mult),
                          r=kft(fT) + kft(R[1]), w=kact(j))
            if hf == 0:
                rmsnorm_tiles(PO_GFFN + 8 * l, [2, 3],
                              lambda c, n: (h2[:, c, 512 * (n % 2):512 * (n % 2 + 1)], kh2(c, n % 2)))
            load_dn(1, 1)

            def fd(e, n2, bi, wv, j0):
                ins = None
                for j in range(j0, j0 + 11):
                    ins = e.matmul(bank(bi), lhsT=wv[:, j - j0, :], rhs=act3[:, j, 512 * n2:512 * (n2 + 1)],
                                   start=(j == 0), stop=(j == 21))
                return ins
            for mp in range(4):
                ms = (2 * mp, 2 * mp + 1)
                bis = {(m, n2): brot.next() for m in ms for n2 in range(2)}
                for m in ms:
                    sd0, wd0 = dn[(m, 0)]
                    for n2 in range(2):
                        P.add("pe", lambda e, n2=n2, bi=bis[(m, n2)], wv=wd0: fd(e, n2, bi, wv, 0),
                              r=[k_ for j in range(11) for k_ in kact(j, n2)] + [kw(sd0)], w=[kb(bis[(m, n2)])])
                if mp + 1 < 4:
                    load_dn(2 * mp + 2, 0)
                    load_dn(2 * mp + 3, 0)
                for m in ms:
                    sd1, wd1 = dn[(m, 1)]
                    for n2 in range(2):
                        P.add("pe", lambda e, n2=n2, bi=bis[(m, n2)], wv=wd1: fd(e, n2, bi, wv, 11),
                              r=[k_ for j in range(11, 22) for k_ in kact(j, n2)] + [kw(sd1)], w=[kb(bis[(m, n2)])])
                        n = 2 * hf + n2
                        P.add("dve", lambda e, m=m, n=n, bi=bis[(m, n2)]: e.tensor_tensor(
                            out=xT[:, m, 512 * n:512 * (n + 1)], in0=xT[:, m, 512 * n:512 * (n + 1)], in1=bank(bi), op=ALU.add),
                            r=[kb(bis[(m, n2)]), kx(m, n)], w=[kx(m, n)])
                if mp + 1 < 4:
                    load_dn(2 * mp + 2, 1)
                    load_dn(2 * mp + 3, 1)

    P.enabled = True
    if dump:
        dq0_d = nc.dram_tensor("dbg_q0", [128, 12 * S], BF16, kind="ExternalOutput").ap()
        P.add("sp", lambda e: e.dma_start(out=dq0_d[:, :], in_=qkv_t[:, :]), r=KR("Q", 0, 49152), w=[("out", 3)], dma=True)
    fin3 = stgF.rearrange("p (c t) -> p c t", c=8)
    ostg = qkv_t[:].bitcast(F32)[:, 0:8 * 1024].rearrange("p (j d) -> p j d", j=8)
    def kfin(c, t0, nt): return KR("H", (c * 1024 + t0) * 4, nt * 4)
    for half in range(2):
        if final_norm:
            rmsnorm_tiles(PO_GFIN, [2 * half, 2 * half + 1],
                          lambda c, n: (fin3[:, c, 512 * (n % 2):512 * (n % 2 + 1)], kfin(c, 512 * (n % 2), 512)))
        for jt in range(8):
            for cg in range(2):
                bi = brot.next()

                def ftr(e, jt=jt, cg=cg, bi=bi, half=half):
                    ins = None
                    for i in range(4):
                        c = cg * 4 + i
                        if final_norm:
                            src = fin3[:, c, 128 * jt:128 * (jt + 1)]
                        else:
                            src = xT[:, c, 1024 * half + 128 * jt:1024 * half + 128 * (jt + 1)]
                        ins = e.transpose(bank(bi)[:, 128 * i:128 * (i + 1)], src, identF_t[:])
                    return ins
                if final_norm:
                    rr = [k_ for i in range(4) for k_ in kfin(cg * 4 + i, 128 * jt, 128)]
                else:
                    rr = [kx(cg * 4 + i, 2 * half + jt // 4) for i in range(4)]
                P.add("pe", ftr, r=rr + ["identF"], w=[kb(bi)])
                dst = ostg[:, jt, 512 * cg:512 * (cg + 1)]
                kd = KR("Q", (jt * 1024 + 512 * cg) * 4, 2048)
                if evrot.next() == "act":
                    P.add("act", lambda e, dst=dst, bi=bi: e.copy(out=dst, in_=bank(bi)), r=[kb(bi)], w=kd)
                else:
                    P.add("dve", lambda e, dst=dst, bi=bi: e.tensor_copy(out=dst, in_=bank(bi)), r=[kb(bi)], w=kd)
        dsto = out_d[half * 1024:(half + 1) * 1024, :].rearrange("(j p) d -> p j d", p=128)
        P.add("sp", lambda e, dsto=dsto: e.dma_start(out=dsto, in_=ostg), r=KR("Q", 0, 32768),
              w=[("out", half)], dma=True)
    if dump:
        dh_d = nc.dram_tensor("dbg_h", [128, 8 * S], BF16, kind="ExternalOutput").ap()
        P.add("sp", lambda e: e.dma_start(out=dh_d[:, :], in_=hT_t[:, :]), r=KH_ALL, w=[("out", 2)], dma=True)
        P.add("sp", lambda e: None, r=[("out", 0), ("out", 1), ("out", 2), ("out", 3)])
    else:
        P.add("sp", lambda e: None, r=[("out", 0), ("out", 1)])

    P.emit(nc, stack)
    stack.close()
    return nc


def _t5_bucket_np(dist):
    num_buckets, max_distance = 32, 2048
    max_exact = num_buckets // 2
    d_f = np.maximum(dist, 1).astype(np.float32)
    large = max_exact + (np.log(d_f / max_exact) / math.log(max_distance / max_exact)
                         * (num_buckets - max_exact)).astype(np.int32)
    large = np.minimum(large, num_buckets - 1)
    return np.where(dist < max_exact, dist, large)


def _fm(a):
    a = np.asarray(a, np.float32)
    lead = a.shape[:-1]
    c = a.shape[-1] // 128
    a = a.reshape(lead + (c, 128))
    a = np.moveaxis(a, -1, 0)
    return np.ascontiguousarray(a).reshape(128, -1)


def _host_tables(inp):
    prm = np.zeros((128, NPRM), np.float32)
    prm[:, PO_GMIX:PO_GMIX + 32] = _fm(inp["norm_mix_g"])
    prm[:, PO_GFFN:PO_GFFN + 32] = _fm(inp["norm_ffn_g"])
    prm[:, PO_GOUT:PO_GOUT + 32] = _fm(inp["out_norm_g"])
    prm[:, PO_GFIN:PO_GFIN + 8] = _fm(inp["final_g"])
    prm[:, PO_CVA:PO_CVA + 24] = _fm(inp["conv_a_w"])
    prm[:, PO_CVC:PO_CVC + 248] = _fm(inp["conv_c_w"])
    prm[:, PO_CVCB:PO_CVCB + 8] = _fm(inp["conv_c_b"])
    prm[:, PO_LNG:PO_LNG + 8] = _fm(inp["ln_c_g"])
    prm[:, PO_LNB:PO_LNB + 8] = _fm(inp["ln_c_b"])
    prm[:, PO_CVF:PO_CVF + 528] = _fm(inp["conv_f_w"])
    rb = np.asarray(inp["rel_bias"], np.float32)
    jk = np.arange(128)[:, None]
    bg = np.zeros((128, 8, E_W), np.float32)
    mk = np.zeros((128, E_W), np.float32)
    for bi, d in enumerate((1, 4, 16)):
        w = 256 if d != 16 else 128
        rel = np.arange(w)[None, :] - jk
        idx = _t5_bucket_np(np.maximum(rel, 0) * d)
        bg[:, :, E_OFF[bi]:E_OFF[bi] + w] = np.transpose(rb[idx], (0, 2, 1))
        mk[:, E_OFF[bi]:E_OFF[bi] + w] = ((rel >= 0) & (rel <= 128)).astype(np.float32)
    return prm, bg.reshape(128, 8 * E_W), mk


_CACHE = {}


def kernel(**inputs):
    inp = {k: np.asarray(v) for k, v in inputs.items()}
    x = np.ascontiguousarray(inp["x"], dtype=np.float32)
    prm, bg, mk = _host_tables(inp)
    if "nc" not in _CACHE:
        _CACHE["nc"] = build_program(DEPTH)
    nc = _CACHE["nc"]
    shared = {
        "w_in": np.ascontiguousarray(inp["w_in"], dtype=np.float32),
        "w_out": np.ascontiguousarray(inp["w_out"], dtype=np.float32),
        "w_up": np.ascontiguousarray(inp["w_up"], dtype=np.float32),
        "w_down": np.ascontiguousarray(inp["w_down"], dtype=np.float32),
        "prm": prm, "biasg": bg, "mask01": mk,
    }
    in_maps = [dict(shared, x=x[b]) for b in range(NCORES)]
    res = run_bass_kernel_spmd(nc, in_maps, core_ids=list(range(NCORES)))
    return np.stack([np.asarray(r["out"], dtype=np.float32) for r in res.results], axis=0)
```

```python
import math
from contextlib import ExitStack

import numpy as np
import concourse.bass as bass
import concourse.mybir as mybir
from concourse.bass_utils import run_bass_kernel_spmd

F32 = mybir.dt.float32
BF16 = mybir.dt.bfloat16
ALU = mybir.AluOpType
AF = mybir.ActivationFunctionType

S = 2048
D = 1024
DEPTH = 4
IN_COLS = 2816
DFF = 2816
EPS = 1e-6
NCORES = 8

PO_GMIX = 0
PO_GFFN = 32
PO_GOUT = 64
PO_GFIN = 96
PO_CVA = 104
PO_CVC = 128
PO_CVCB = 376
PO_LNG = 384
PO_LNB = 392
PO_HLNB = 400
PO_CVF = 408
NPRM = 408 + 528


class _Op:
    __slots__ = ("eng", "fn", "reads", "writes", "dma", "waits", "sig", "need_sig", "deps")


class Prog:
    ENGS = ("pe", "act", "dve", "pool", "sp")

    def __init__(self):
        self.ops = []
        self.enabled = True

    def add(self, eng, fn, r=(), w=(), dma=False):
        if not self.enabled:
            return None
        o = _Op()
        o.eng, o.fn, o.reads, o.writes, o.dma = eng, fn, tuple(r), tuple(w), dma
        o.waits, o.sig, o.need_sig, o.deps = [], None, False, ()
        self.ops.append(o)
        return o

    def resolve(self):
        ops = self.ops
        last_w = {}
        rd_eng = {}
        rd_dma = {}
        for i, o in enumerate(ops):
            deps = set()
            for k in o.reads:
                j = last_w.get(k)
                if j is not None:
                    deps.add(j)
            for k in o.writes:
                j = last_w.get(k)
                if j is not None:
                    deps.add(j)
                d = rd_eng.get(k)
                if d:
                    deps.update(d.values())
                d2 = rd_dma.get(k)
                if d2:
                    deps.update(d2)
            deps.discard(i)
            o.deps = tuple(j for j in deps if ops[j].dma or ops[j].eng != o.eng)
            for j in o.deps:
                ops[j].need_sig = True
            for k in o.writes:
                last_w[k] = i
                rd_eng.pop(k, None)
                rd_dma.pop(k, None)
            wset = set(o.writes)
            for k in o.reads:
                if k in wset:
                    continue
                if o.dma:
                    rd_dma.setdefault(k, []).append(i)
                else:
                    rd_eng.setdefault(k, {})[o.eng] = i

    def emit(self, nc, stack, ndma=12):
        self.resolve()
        ops = self.ops
        sems = {e: stack.enter_context(nc.semaphore("sem_" + e)) for e in self.ENGS}
        dsems = {q: [stack.enter_context(nc.semaphore("dsem_%s_%d" % (q, i))) for i in range(ndma)]
                 for q in ("sp", "pool")}
        cnt = {e: 0 for e in self.ENGS}
        dcnt = {"sp": 0, "pool": 0}
        pre = {}
        for i, o in enumerate(ops):
            if o.dma:
                k = dcnt[o.eng]
                dcnt[o.eng] += 1
                sem = dsems[o.eng][k % ndma]
                o.sig = (sem, 16 * (k // ndma + 1))
                if k >= ndma:
                    pre[i] = (sem, 16 * (k // ndma))
            elif o.need_sig:
                cnt[o.eng] += 1
                o.sig = (sems[o.eng], cnt[o.eng])
        waited = {e: {} for e in self.ENGS}
        for i, o in enumerate(ops):
            need = {}
            if i in pre:
                s, v = pre[i]
                need[id(s)] = (s, v)
            for j in o.deps:
                s, v = ops[j].sig
                cur = need.get(id(s))
                if cur is None or cur[1] < v:
                    need[id(s)] = (s, v)
            wl = []
            wd = waited[o.eng]
            for sid, (s, v) in need.items():
                if wd.get(sid, 0) >= v:
                    continue
                wd[sid] = v
                wl.append((s, v))
            o.waits = wl
        by_eng = {e: [o for o in ops if o.eng == e] for e in self.ENGS}

        def run(handle, name):
            for o in by_eng[name]:
                for s, v in o.waits:
                    handle.wait_ge(s, v)
                inst = o.fn(handle)
                if o.sig is not None:
                    assert inst is not None
                    inst.then_inc(o.sig[0], 16 if o.dma else 1)

        with nc.Block() as block:
            @block.tensor
            def _(e):
                run(e, "pe")

            @block.scalar
            def _(e):
                run(e, "act")

            @block.vector
            def _(e):
                run(e, "dve")

            @block.gpsimd
            def _(e):
                run(e, "pool")

            @block.sync
            def _(e):
                run(e, "sp")


class Rot:
    def __init__(self, items):
        self.items = list(items)
        self.i = 0

    def next(self):
        v = self.items[self.i % len(self.items)]
        self.i += 1
        return v


def attn_items(hf):
    items = []
    for bi, d in enumerate((1, 4, 16)):
        L = S // d
        Lh = 1024 // d
        qlo, qhi = Lh * hf, Lh * (hf + 1)
        for r in range(d):
            for kt in range(L // 128):
                a = max(128 * kt, qlo)
                b = min(128 * kt + 256, qhi, L)
                if b <= a:
                    continue
                kp = min(128, b - 128 * kt)
                items.append((bi, d, r, 128 * kt, kp, a, b - a, a - 128 * kt))
    return items


E_OFF = (0, 256, 512)
E_W = 640


def build_program(n_layers, final_norm=True, phases="NACQTBF", dump=False):
    nc = bass.Bass("TRN2", target_bir_lowering=False)
    P = Prog()
    stack = ExitStack()

    x_d = nc.dram_tensor("x", [S, D], F32, kind="ExternalInput").ap()
    w_in_d = nc.dram_tensor("w_in", [DEPTH, D, IN_COLS], F32, kind="ExternalInput").ap()
    w_out_d = nc.dram_tensor("w_out", [DEPTH, D, D], F32, kind="ExternalInput").ap()
    w_up_d = nc.dram_tensor("w_up", [DEPTH, D, 2 * DFF], F32, kind="ExternalInput").ap()
    w_dn_d = nc.dram_tensor("w_down", [DEPTH, DFF, D], F32, kind="ExternalInput").ap()
    prm_d = nc.dram_tensor("prm", [128, NPRM], F32, kind="ExternalInput").ap()
    bg_d = nc.dram_tensor("biasg", [128, 8 * E_W], F32, kind="ExternalInput").ap()
    mk_d = nc.dram_tensor("mask01", [128, E_W], F32, kind="ExternalInput").ap()
    out_d = nc.dram_tensor("out", [S, D], F32, kind="ExternalOutput").ap()
    vscr_d = nc.dram_tensor("vscr", [S, 512], BF16, kind="Internal").ap()

    sb = lambda name, shape, dt: stack.enter_context(nc.sbuf_tensor(name, shape, dt))
    xT_t = sb("xT", [128, 8 * S], F32)
    hT_t = sb("hT", [128, 8 * S], BF16)
    qkv_t = sb("qkv", [128, 12 * S], BF16)
    yac_t = sb("yac", [128, 2 * S], BF16)
    E_t = sb("E", [128, 8 * E_W], BF16)
    WSLOT = 2048
    NW = 5
    wr_t = sb("wring", [128, NW * WSLOT], BF16)
    NSM = 6
    sm_t = sb("sm", [128, NSM * 512], F32)
    NSQ = 3
    sq_t = sb("sq", [128, NSQ * 512], BF16)
    prm_t = sb("prm_sb", [128, NPRM], F32)
    identF_t = sb("identF", [128, 128], F32)
    identB_t = sb("identB", [128, 128], BF16)
    onesB_t = sb("onesB", [128, 128], BF16)
    onesF_t = sb("onesF", [128, 64], F32)
    NDG = 6
    dg_t = sb("diag", [128, NDG * 128], BF16)
    halo_t = sb("halo", [128, 44 * 2], F32)
    ps_t = stack.enter_context(nc.psum_tensor("ps", [128, 8 * 512], F32))

    xT = xT_t[:].rearrange("p (c t) -> p c t", c=8)
    hT = hT_t[:].rearrange("p (c t) -> p c t", c=8)
    q4 = qkv_t[:, 0:4 * S].rearrange("p (c t) -> p c t", c=4)
    k4 = qkv_t[:, 4 * S:8 * S].rearrange("p (c t) -> p c t", c=4)
    v4 = qkv_t[:, 8 * S:12 * S].rearrange("p (c t) -> p c t", c=4)
    yac = yac_t[:].rearrange("p (c t) -> p c t", c=2)
    pbuf_t = qkv_t[:, 4 * S:4 * S + 2 * (2 + S)].bitcast(F32)
    E3 = E_t[:].rearrange("p (h w) -> p h w", h=8)
    prm = prm_t
    bank = lambda i: ps_t[:, 512 * i:512 * (i + 1)]

    def pcol(off):
        return prm[:, off:off + 1]

    GB = 1024

    def KR(region, b0, nb):
        return [(region, g) for g in range(b0 // GB, (b0 + nb - 1) // GB + 1)]

    def kx(c, n): return ("x", c, n)
    def kh(c, n): return ("H", c * 4 + n)
    KH_ALL = KR("H", 0, 32768)
    QOFF = {"q": 0, "k": 16384, "v": 32768}
    def kqkv(which, p, n): return ("Q", (QOFF[which] + (p * 2048 + n * 512) * 2) // GB)
    def kqkv_rng(which, p, t0, t1): return KR("Q", QOFF[which] + (p * 2048 + t0) * 2, (t1 - t0) * 2)
    def kb(i): return ("ps", i)
    def kyac(c, n): return ("Y", c * 4 + n)

    def kyac_all(): return KR("Y", 0, 8192)
    vt_t = sb("vt", [128, 4 * 260], BF16)

    P.add("sp", lambda e: e.dma_start(out=prm[:, :], in_=prm_d[:, :]), w=["prm"], dma=True)

    def f_ident(e):
        e.memset(identF_t[:], 0.0)
        return e.affine_select(out=identF_t[:], in_=identF_t[:], pattern=[[-1, 128]], compare_op=ALU.not_equal,
                               fill=1.0, base=0, channel_multiplier=1)
    P.add("pool", f_ident, w=["identF"])
    P.add("dve", lambda e: e.tensor_copy(out=identB_t[:], in_=identF_t[:]), r=["identF"], w=["identB"])
    P.add("dve", lambda e: e.memset(onesB_t[:], 1.0), w=["onesB"])
    P.add("dve", lambda e: e.memset(onesF_t[:], 1.0), w=["onesF"])
    P.add("dve", lambda e: e.memset(vt_t[:], 1.0), w=[("vt", i) for i in range(4)])
    P.add("dve", lambda e: e.tensor_scalar(out=prm[:, PO_HLNB:PO_HLNB + 8], in0=prm[:, PO_LNB:PO_LNB + 8],
                                           scalar1=0.5, scalar2=None, op0=ALU.mult), r=["prm"], w=["prm"])
    stgF = hT_t[:].bitcast(F32)
    stgQ = qkv_t[:].bitcast(F32)
    stg3 = stgF.rearrange("p (j d) -> p j d", j=8)
    stg3b = stgQ[:, 0:8192].rearrange("p (j d) -> p j d", j=8)
    P.add("sp", lambda e: e.dma_start(out=stg3, in_=x_d[0:1024, :].rearrange("(j p) d -> p j d", p=128)),
          w=KH_ALL, dma=True)
    bgv = stgQ[:, 0:8 * E_W]
    mkv = stgQ[:, 8 * E_W:9 * E_W]
    K_BG = KR("Q", 0, 8 * E_W * 4)
    K_MK = KR("Q", 8 * E_W * 4, E_W * 4)
    P.add("sp", lambda e: e.dma_start(out=bgv, in_=bg_d[:, :]), w=K_BG, dma=True)
    P.add("sp", lambda e: e.dma_start(out=mkv, in_=mk_d[:, :]), w=K_MK, dma=True)
    P.add("act", lambda e: e.activation(out=bgv, in_=bgv, func=AF.Exp), r=K_BG, w=K_BG)

    def f_E(e):
        ins = None
        bg3 = bgv.rearrange("p (h w) -> p h w", h=8)
        for h in range(8):
            ins = e.tensor_tensor(out=E3[:, h, :], in0=bg3[:, h, :], in1=mkv, op=ALU.mult)
        return ins
    P.add("dve", f_E, r=K_BG + K_MK, w=["E"])

    brot = Rot(range(8))
    evrot = Rot(["act", "dve"])
    for half in range(2):
        stg_h = stg3 if half == 0 else stg3b
        reg_h = "H" if half == 0 else "Q"
        if half == 1:
            src = x_d[1024:2048, :].rearrange("(j p) d -> p j d", p=128)
            P.add("sp", lambda e, src=src: e.dma_start(out=stg3b, in_=src), w=KR("Q", 0, 32768), dma=True)
        for c in range(8):
            for g in range(2):
                bi = brot.next()

                def f_tr(e, c=c, g=g, bi=bi, stg_h=stg_h):
                    ins = None
                    for i in range(4):
                        ins = e.transpose(bank(bi)[:, 128 * i:128 * (i + 1)], stg_h[:, g * 4 + i, c * 128:(c + 1) * 128],
                                          identF_t[:])
                    return ins
                P.add("pe", f_tr, r=KR(reg_h, g * 4 * 4096, 4 * 4096) + ["identF"], w=[kb(bi)])
                n = half * 2 + g
                dst = xT[:, c, n * 512:(n + 1) * 512]
                if evrot.next() == "act":
                    P.add("act", lambda e, dst=dst, bi=bi: e.copy(out=dst, in_=bank(bi)), r=[kb(bi)], w=[kx(c, n)])
                else:
                    P.add("dve", lambda e, dst=dst, bi=bi: e.tensor_copy(out=dst, in_=bank(bi)), r=[kb(bi)], w=[kx(c, n)])

    smrot = Rot(range(NSM))
    sqrot = Rot(range(NSQ))
    wrot = Rot(range(NW))
    dgrot = Rot(range(NDG))

    def sm(i): return sm_t[:, 512 * i:512 * (i + 1)]
    def ksm(i): return ("sm", i)
    def sq(i): return sq_t[:, 512 * i:512 * (i + 1)]
    def ksq(i): return ("sq", i)
    def wslot(i): return wr_t[:, WSLOT * i:WSLOT * (i + 1)]
    def kw(i): return ("w", i)

    def load_w(src_ap, shape_kc, ncols):
        si = wrot.next()
        assert shape_kc * ncols <= WSLOT
        dst = wslot(si)[:, 0:shape_kc * ncols].rearrange("p (k n) -> p k n", k=shape_kc)
        P.add("pool", lambda e: e.dma_start(out=dst, in_=src_ap.rearrange("(k p) n -> p k n", p=128)),
              w=[kw(si)], dma=True)
        return si, dst

    def rstd_from_ss(bi, scale):
        si = smrot.next()

        def f(e):
            e.activation(out=sm(si), in_=bank(bi), func=AF.Ln, bias=EPS, scale=scale)
            return e.activation(out=sm(si), in_=sm(si), func=AF.Exp, scale=-0.5)
        P.add("act", f, r=[kb(bi)], w=[ksm(si)])
        return si

    def rmsnorm_tiles(gcol, token_tiles, dst_of):
        for n in token_tiles:
            bi = brot.next()
            for c in range(8):
                qi = sqrot.next()
                P.add("act", lambda e, c=c, n=n, qi=qi: e.activation(out=sq(qi), in_=xT[:, c, n * 512:(n + 1) * 512],
                                                                     func=AF.Square),
                      r=[kx(c, n)], w=[ksq(qi)])
                P.add("pe", lambda e, c=c, qi=qi, bi=bi: e.matmul(bank(bi), lhsT=onesB_t[:], rhs=sq(qi),
                                                                  start=(c == 0), stop=(c == 7)),
                      r=[ksq(qi), "onesB"], w=[kb(bi)])
            si = rstd_from_ss(bi, 1.0 / D)
            for c in range(8):
                dap, dkeys = dst_of(c, n)
                P.add("dve", lambda e, c=c, n=n, si=si, dap=dap: e.scalar_tensor_tensor(
                    out=dap, in0=xT[:, c, n * 512:(n + 1) * 512], scalar=pcol(gcol + c), in1=sm(si),
                    op0=ALU.mult, op1=ALU.mult), r=[kx(c, n), ksm(si), "prm"], w=dkeys)

    def hrhs(kc, n):
        return hT[:, kc, n * 512:(n + 1) * 512]

    def mm8(wv, col0, rhs_fn, n, bi):
        def f(e):
            ins = None
            for kc in range(8):
                ins = e.matmul(bank(bi), lhsT=wv[:, kc, col0:col0 + 128], rhs=rhs_fn(kc, n),
                               start=(kc == 0), stop=(kc == 7))
            return ins
        return f

    for l in range(n_layers):
        P.enabled = "N" in phases
        rmsnorm_tiles(PO_GMIX + 8 * l, range(4), lambda c, n: (hT[:, c, n * 512:(n + 1) * 512], [kh(c, n)]))
        KHN = lambda n: [kh(kc, n) for kc in range(8)]

        P.enabled = "A" in phases
        sA = {}
        for name, col in (("h", 0), ("b", 256), ("c", 512)):
            sA[name] = load_w(w_in_d[l, :, col:col + 256], 8, 256)
        brot = Rot(range(8))
        pendA = []
        def kpb(t0, nt): return KR("Q", 16384 + t0 * 4, nt * 4)
        P.add("dve", lambda e: e.memset(pbuf_t[:, 0:2], 0.0), w=kpb(0, 2))
        for cc in range(2):
            for n in range(4):
                si_w, wv = sA["h"]
                bi = brot.next()
                P.add("pe", mm8(wv, cc * 128, hrhs, n, bi), r=KHN(n) + [kw(si_w)], w=[kb(bi)])
                si = smrot.next()
                P.add("act", lambda e, si=si, bi=bi: e.copy(out=sm(si), in_=bank(bi)), r=[kb(bi)], w=[ksm(si)])
                si_c, wc = sA["c"]
                bi2 = brot.next()
                P.add("pe", mm8(wc, cc * 128, hrhs, n, bi2), r=KHN(n) + [kw(si_c)], w=[kb(bi2)])
                P.add("dve", lambda e, n=n, bi2=bi2, si=si: e.tensor_tensor(
                    out=pbuf_t[:, 2 + 512 * n:2 + 512 * (n + 1)], in0=bank(bi2), in1=sm(si), op=ALU.mult),
                    r=[kb(bi2), ksm(si)], w=kpb(2 + 512 * n, 512))
            si_b, wb = sA["b"]
            for n in range(4):
                bi = brot.next()
                P.add("pe", mm8(wb, cc * 128, hrhs, n, bi), r=KHN(n) + [kw(si_b)], w=[kb(bi)])
                si = smrot.next()
                cw = PO_CVA + (l * 3) * 2 + cc
                pk = kpb(512 * n, 514)
                P.add("act", lambda e, n=n, si=si, cw=cw: e.activation(
                    out=sm(si), in_=pbuf_t[:, 2 + 512 * n:2 + 512 * (n + 1)], func=AF.Identity, scale=pcol(cw + 4)),
                    r=pk + ["prm"], w=[ksm(si)])

                def f4(e, n=n, si=si, cw=cw):
                    e.scalar_tensor_tensor(out=sm(si), in0=pbuf_t[:, 1 + 512 * n:1 + 512 * (n + 1)], scalar=pcol(cw + 2),
                                           in1=sm(si), op0=ALU.mult, op1=ALU.add)
                    return e.scalar_tensor_tensor(out=sm(si), in0=pbuf_t[:, 512 * n:512 * (n + 1)], scalar=pcol(cw),
                                                  in1=sm(si), op0=ALU.mult, op1=ALU.add)
                P.add("dve", f4, r=pk + [ksm(si), "prm"], w=[ksm(si)])
                P.add("dve", lambda e, n=n, si=si, bi=bi, cc=cc: e.tensor_tensor(
                    out=yac[:, cc, 512 * n:512 * (n + 1)], in0=bank(bi), in1=sm(si), op=ALU.mult),
                    r=[kb(bi), ksm(si)], w=[kyac(cc, n)])
                pendA.append((cc, n))

        def finish_group(ss_banks, scale_ss, row0, post_scale, mid=None):
            rs = {}
            for n in range(4):
                rs[n] = rstd_from_ss(ss_banks[n], scale_ss)
            for n in range(4):
                for cc in range(2):
                    P.add("dve", lambda e, n=n, cc=cc: e.scalar_tensor_tensor(
                        out=yac[:, cc, 512 * n:512 * (n + 1)], in0=yac[:, cc, 512 * n:512 * (n + 1)], scalar=post_scale,
                        in1=sm(rs[n]), op0=ALU.mult, op1=ALU.mult), r=[kyac(cc, n), ksm(rs[n])], w=[kyac(cc, n)])
            si_w, wv = load_w(w_out_d[l, row0:row0 + 256, :], 2, 1024)
            for kc in range(2):
                gc = PO_GOUT + 8 * l + row0 // 128 + kc
                P.add("dve", lambda e, kc=kc, gc=gc, wv=wv: e.tensor_scalar(
                    out=wv[:, kc, :], in0=wv[:, kc, :], scalar1=pcol(gc), scalar2=None, op0=ALU.mult),
                    r=[kw(si_w), "prm"], w=[kw(si_w)])
            if mid is not None:
                en = P.enabled
                mid()
                P.enabled = en
            for m in range(8):
                for n in range(4):
                    bi = brot.next()

                    def f(e, m=m, n=n, bi=bi, wv=wv):
                        e.matmul(bank(bi), lhsT=wv[:, 0, m * 128:(m + 1) * 128], rhs=yac[:, 0, 512 * n:512 * (n + 1)],
                                 start=True, stop=False)
                        return e.matmul(bank(bi), lhsT=wv[:, 1, m * 128:(m + 1) * 128],
                                        rhs=yac[:, 1, 512 * n:512 * (n + 1)], start=False, stop=True)
                    P.add("pe", f, r=[kyac(0, n), kyac(1, n), kw(si_w)], w=[kb(bi)])
                    P.add("dve", lambda e, m=m, n=n, bi=bi: e.tensor_tensor(
                        out=xT[:, m, 512 * n:512 * (n + 1)], in0=xT[:, m, 512 * n:512 * (n + 1)], in1=bank(bi),
                        op=ALU.add), r=[kb(bi), kx(m, n)], w=[kx(m, n)])

        UW = 30 + S
        u3 = qkv_t[:, 0:2 * UW].rearrange("p (c t) -> p c t", c=2)
        def ku(cc, t0, nt): return KR("Q", (cc * UW + t0) * 2, nt * 2)

        def c_head():
            P.enabled = "C" in phases
            for cc in range(2):
                P.add("dve", lambda e, cc=cc: e.memset(u3[:, cc, 0:30], 0.0), w=ku(cc, 0, 30))
            sC = {}
            for name, col in (("val", 2304), ("gate", 2560)):
                sC[name] = load_w(w_in_d[l, :, col:col + 256], 8, 256)
            for cc in range(2):
                for n in range(4):
                    si_g, wg = sC["gate"]
                    si_v, wvv = sC["val"]
                    bg_, bv_ = brot.next(), brot.next()
                    P.add("pe", mm8(wg, cc * 128, hrhs, n, bg_), r=KHN(n) + [kw(si_g)], w=[kb(bg_)])
                    P.add("pe", mm8(wvv, cc * 128, hrhs, n, bv_), r=KHN(n) + [kw(si_v)], w=[kb(bv_)])
                    si = smrot.next()
                    P.add("act", lambda e, si=si, b=bg_: e.activation(out=sm(si), in_=bank(b), func=AF.Tanh, scale=0.5),
                          r=[kb(bg_)], w=[ksm(si)])
                    P.add("dve", lambda e, si=si, b=bv_, n=n, cc=cc: e.scalar_tensor_tensor(
                        out=u3[:, cc, 30 + 512 * n:30 + 512 * (n + 1)], in0=sm(si), scalar=1.0, in1=bank(b),
                        op0=ALU.add, op1=ALU.mult), r=[ksm(si), kb(bv_)], w=ku(cc, 30 + 512 * n, 512))

        P.enabled = "A" in phases
        ssA = [brot.next() for _ in range(4)]
        for n in range(4):
            for cc in range(2):
                qi = sqrot.next()
                P.add("act", lambda e, n=n, qi=qi, cc=cc: e.activation(out=sq(qi), in_=yac[:, cc, 512 * n:512 * (n + 1)],
                                                                       func=AF.Square), r=[kyac(cc, n)], w=[ksq(qi)])
                P.add("pe", lambda e, bk=ssA[n], qi=qi, cc=cc: e.matmul(bank(bk), lhsT=onesB_t[:], rhs=sq(qi),
                                                                      start=(cc == 0), stop=(cc == 1)),
                      r=[ksq(qi), "onesB"], w=[kb(ssA[n])])
        finish_group(ssA, 1.0 / 256, 0, 1.0, mid=c_head)
        NPT = 6
        PTW = 1024
        pt_v = hT_t[:, 0:NPT * PTW]
        OSB0 = NPT * PTW * 2
        osb_v = hT_t[:, NPT * PTW:NPT * PTW + 6 * 1024].bitcast(F32)
        ptrot = Rot(range(NPT))
        vtrot = Rot(range(4))

        def pt4(i): return pt_v[:, PTW * i:PTW * (i + 1)].rearrange("p (h i q) -> p h i q", h=2, i=2)
        def kpt(i, hh): return KR("H", PTW * 2 * i + 1024 * hh, 1024)
        def vt4(i): return vt_t[:, 260 * i:260 * (i + 1)].rearrange("p (i h d) -> p i h d", i=2, h=2)
        def osb(hh): return osb_v[:, 1024 * hh:1024 * (hh + 1)]
        def kosb(hh, j): return KR("H", OSB0 + hh * 4096 + j * 2048, 2048)
        rbc_v = osb_v[:, 2048:3072]
        K_RD = KR("H", OSB0 + 8192, 4096)
        VPW = 16 * 2 * 128
        VP2_EL = NPT * PTW + 6 * 1024
        assert VP2_EL + VPW <= 8 * S

        def vp_flat(b_):
            if b_ < 2:
                return qkv_t[:, 8 * S + VPW * b_:8 * S + VPW * (b_ + 1)]
            return hT_t[:, VP2_EL:VP2_EL + VPW]
        def vp4(b_): return vp_flat(b_).rearrange("p (t h d) -> p t h d", t=16, h=2)
        def kvp(b_): return KR("Q", 32768 + VPW * 2 * b_, VPW * 2) if b_ < 2 else KR("H", VP2_EL * 2, VPW * 2)
        VP_ALL = [("vp", b_, g_, h_) for b_ in range(3) for g_ in range(4) for h_ in range(2)]
        VP_01 = [k_ for k_ in VP_ALL if k_[1] < 2]
        VP_2 = [k_ for k_ in VP_ALL if k_[1] == 2]
        VSCR_ALL = [("vscr", tt) for tt in range(16)]

        def load_vp(p, b_):
            cols = slice(128 * p, 128 * (p + 1))
            if b_ == 0:
                src = vscr_d[:, cols].rearrange("(t q) c -> q t c", q=128)
            elif b_ == 1:
                src = vscr_d[:, cols].rearrange("(n q r) c -> q r n c", q=128, r=4)
            else:
                src = vscr_d[:, cols].rearrange("(q r) c -> q r c", r=16)
            for t0 in range(0, 16, 4):
                if b_ == 1:
                    sap = src[:, t0 // 4, :, :]
                else:
                    sap = src[:, t0:t0 + 4, :]
                for hh in range(2):
                    dap = vp4(b_)[:, t0:t0 + 4, hh, 0:64]
                    P.add("sp", lambda e, sap=sap, dap=dap, hh=hh: e.dma_start(out=dap, in_=sap[:, :, 64 * hh:64 * (hh + 1)]),
                          r=VSCR_ALL, w=[("vp", b_, t0 // 4, hh)], dma=True)
        P.enabled = "Q" in phases
        vsl = [load_w(w_in_d[l, :, 1792 + 256 * sl_:1792 + 256 * (sl_ + 1)], 8, 256) for sl_ in range(2)]
        for tt in range(16):
            bi = brot.next()

            def fv(e, tt=tt, bi=bi, vsl=vsl):
                ins = None
                for sl_ in range(2):
                    wv = vsl[sl_][1]
                    for kc in range(8):
                        ins = e.matmul(bank(bi)[:, 256 * sl_:256 * (sl_ + 1)], lhsT=hT[:, kc, 128 * tt:128 * (tt + 1)],
                                       rhs=wv[:, kc, :], start=(kc == 0), stop=(kc == 7), skip_group_check=True)
                return ins
            P.add("pe", fv, r=KHN(tt // 4) + [kw(vsl[0][0]), kw(vsl[1][0])], w=[kb(bi)])
            vst = yac_t[:, 512 * (tt % 8):512 * (tt % 8 + 1)]
            kvst = [("Y", tt % 8)]
            if evrot.next() == "act":
                P.add("act", lambda e, vst=vst, bi=bi: e.copy(out=vst, in_=bank(bi)), r=[kb(bi)], w=kvst)
            else:
                P.add("dve", lambda e, vst=vst, bi=bi: e.tensor_copy(out=vst, in_=bank(bi)), r=[kb(bi)], w=kvst)
            P.add("sp", lambda e, vst=vst, tt=tt: e.dma_start(out=vscr_d[128 * tt:128 * (tt + 1), :], in_=vst),
                  r=kvst, w=[("vscr", tt)], dma=True)
        P.add("dve", lambda e: e.memset(qkv_t[:, 8 * S:8 * S + 2 * VPW], 1.0), w=kvp(0) + kvp(1) + VP_01)
        load_vp(0, 0)
        load_vp(0, 1)
        P.enabled = "C" in phases
        brot = Rot(range(4))
        convb = [4, 5, 6, 7]
        for cc in range(2):
            for k in range(31):
                di = dgrot.next()
                wc = PO_CVC + (l * 31 + k) * 2 + cc
                P.add("dve", lambda e, di=di, wc=wc: e.tensor_scalar(
                    out=dg_t[:, 128 * di:128 * (di + 1)], in0=identB_t[:], scalar1=pcol(wc), scalar2=0.5,
                    op0=ALU.mult, op1=ALU.mult), r=["identB", "prm"], w=[("dg", di)])
                for n in range(4):
                    P.add("pe", lambda e, di=di, n=n, k=k, cc=cc: e.matmul(
                        bank(convb[n]), lhsT=dg_t[:, 128 * di:128 * (di + 1)], rhs=u3[:, cc, 512 * n + k:512 * n + k + 512],
                        start=(k == 0), stop=(k == 30)), r=ku(cc, 512 * n + k, 512) + [("dg", di)], w=[kb(convb[n])])
            for n in range(4):
                bc_ = PO_CVCB + 2 * l + cc
                P.add("act", lambda e, n=n, cc=cc, bc_=bc_: e.activation(
                    out=yac[:, cc, 512 * n:512 * (n + 1)], in_=bank(convb[n]), func=AF.Identity, bias=pcol(bc_), scale=1.0),
                    r=[kb(convb[n]), "prm"], w=[kyac(cc, n)])
        def qkv_units():
            for which, col0, dst4, scl in (("q", 768, q4, 0.125), ("k", 1280, k4, 1.0)):
                for sl in range(2):
                    si_w, wv = load_w(w_in_d[l, :, col0 + 256 * sl:col0 + 256 * (sl + 1)], 8, 256)
                    for j in range(2):
                        p = sl * 2 + j
                        for n in range(4):
                            bi = brot.next()
                            P.add("pe", mm8(wv, j * 128, hrhs, n, bi), r=KHN(n) + [kw(si_w)], w=[kb(bi)])
                            dst = dst4[:, p, 512 * n:512 * (n + 1)]
                            if evrot.next() == "act":
                                P.add("act", lambda e, dst=dst, bi=bi, scl=scl: e.activation(out=dst, in_=bank(bi), func=AF.Copy,
                                                                                             scale=scl),
                                      r=[kb(bi)], w=[kqkv(which, p, n)])
                            else:
                                P.add("dve", lambda e, dst=dst, bi=bi, scl=scl: e.tensor_scalar(
                                    out=dst, in0=bank(bi), scalar1=scl, scalar2=None, op0=ALU.mult),
                                    r=[kb(bi)], w=[kqkv(which, p, n)])
                            yield
        qkv_gen = qkv_units()

        def qkv_emit(k):
            en = P.enabled
            P.enabled = "Q" in phases
            for _ in range(k):
                next(qkv_gen, None)
            P.enabled = en

        ssC = [4, 5, 6, 7]
        pend_ss = []
        for n in range(4):
            b1, b2 = brot.next(), brot.next()
            for cc in range(2):
                P.add("pe", lambda e, n=n, cc=cc, b1=b1: e.matmul(bank(b1), lhsT=onesB_t[:], rhs=yac[:, cc, 512 * n:512 * (n + 1)],
                                                                  start=(cc == 0), stop=(cc == 1)),
                      r=[kyac(cc, n), "onesB"], w=[kb(b1)])
                qi = sqrot.next()
                P.add("act", lambda e, n=n, cc=cc, qi=qi: e.activation(out=sq(qi), in_=yac[:, cc, 512 * n:512 * (n + 1)],
                                                                       func=AF.Square), r=[kyac(cc, n)], w=[ksq(qi)])
                P.add("pe", lambda e, cc=cc, b2=b2, qi=qi: e.matmul(bank(b2), lhsT=onesB_t[:], rhs=sq(qi),
                                                                    start=(cc == 0), stop=(cc == 1)),
                      r=[ksq(qi), "onesB"], w=[kb(b2)])
            s_m, s_v = smrot.next(), smrot.next()
            P.add("dve", lambda e, s_m=s_m, b1=b1: e.tensor_scalar(out=sm(s_m), in0=bank(b1), scalar1=1.0 / 256,
                                                                   scalar2=None, op0=ALU.mult),
                  r=[kb(b1)], w=[ksm(s_m)])
            P.add("dve", lambda e, s_m=s_m, s_v=s_v: e.tensor_tensor(out=sm(s_v), in0=sm(s_m), in1=sm(s_m), op=ALU.mult),
                  r=[ksm(s_m)], w=[ksm(s_v)])
            P.add("dve", lambda e, s_v=s_v, b2=b2: e.scalar_tensor_tensor(
                out=sm(s_v), in0=bank(b2), scalar=1.0 / 256, in1=sm(s_v), op0=ALU.mult, op1=ALU.subtract),
                r=[kb(b2), ksm(s_v)], w=[ksm(s_v)])
            def frs(e, s_v=s_v):
                e.activation(out=sm(s_v), in_=sm(s_v), func=AF.Ln, bias=EPS, scale=1.0)
                return e.activation(out=sm(s_v), in_=sm(s_v), func=AF.Exp, scale=-0.5)
            P.add("act", frs, r=[ksm(s_v)], w=[ksm(s_v)])
            for cc in range(2):
                s_d, s_t = smrot.next(), smrot.next()
                gcol = PO_LNG + 2 * l + cc
                bcol = PO_LNB + 2 * l + cc
                hbcol = PO_HLNB + 2 * l + cc

                def fz(e, n=n, cc=cc, s_d=s_d, s_m=s_m, s_v=s_v, gcol=gcol):
                    e.tensor_tensor(out=sm(s_d), in0=yac[:, cc, 512 * n:512 * (n + 1)], in1=sm(s_m), op=ALU.subtract)
                    return e.scalar_tensor_tensor(out=sm(s_d), in0=sm(s_d), scalar=pcol(gcol), in1=sm(s_v),
                                                  op0=ALU.mult, op1=ALU.mult)
                P.add("dve", fz, r=[kyac(cc, n), ksm(s_m), ksm(s_v), "prm"], w=[ksm(s_d)])
                P.add("act", lambda e, n=n, cc=cc, s_d=s_d, bcol=bcol: e.activation(
                    out=yac[:, cc, 512 * n:512 * (n + 1)], in_=sm(s_d), func=AF.Silu, bias=pcol(bcol), scale=1.0),
                    r=[ksm(s_d), "prm"], w=[kyac(cc, n)])
                qi = sqrot.next()
                P.add("act", lambda e, n=n, cc=cc, qi=qi: e.activation(out=sq(qi), in_=yac[:, cc, 512 * n:512 * (n + 1)],
                                                                       func=AF.Square), r=[kyac(cc, n)], w=[ksq(qi)])
                pend_ss.append((n, cc, qi))
            qkv_emit(7)
            for (n_, cc_, qi_) in pend_ss:
                P.add("pe", lambda e, n=n_, cc=cc_, qi=qi_: e.matmul(bank(ssC[n]), lhsT=onesB_t[:], rhs=sq(qi),
                                                                     start=(cc == 0), stop=(cc == 1)),
                      r=[ksq(qi_), "onesB"], w=[kb(ssC[n_])])
            pend_ss = []
        finish_group(ssC, 1.0 / 256, 768, 1.0, mid=lambda: qkv_emit(48))
        brot = Rot(range(8))

        P.enabled = "Q" in phases
        qkv_emit(48)
        brot = Rot(range(8))

        P.enabled = "T" in phases
        P.add("dve", lambda e: e.memset(hT_t[:, VP2_EL:VP2_EL + VPW], 1.0), w=kvp(2) + VP_2)
        load_vp(0, 2)
        XB = {0: 4, 1: 5}

        deferred = []
        pq = []
        gcount = [0]

        def pq_drain(limit):
            while sum(1 for k_, _ in pq if k_ == "pv") > limit:
                pq.pop(0)[1]()
            while pq and pq[0][0] != "pv":
                pq.pop(0)[1]()
        for p in range(4):
            for hf in range(2):
                items = attn_items(hf)
                groups = []
                for it in items:
                    sig = (it[0], it[4], it[6], it[7])
                    if groups and len(groups[-1]) < 2 and groups[-1][0][1] == sig:
                        groups[-1].append((it, sig))
                    else:
                        groups.append([(it, sig)])
                started = set()
                LAG = 3

                def emit_pv(grp, pi, vi, hf=hf, started=started):
                    for ii, (it, sig) in enumerate(grp):
                        (bi_, d, r, ki0, kp, qi0, nq, qoff) = it
                        vtile = (ki0 // 128) if d == 1 else ((r * 4 + ki0 // 128) if d == 4 else r)
                        per = 512 // d
                        s0 = qi0
                        while s0 < qi0 + nq:
                            e0 = min(qi0 + nq, (s0 // per + 1) * per)
                            col = s0 * d + r - 1024 * hf
                            bsub = col // 512
                            for hh in range(2):
                                ab = 2 * hh + bsub
                                first = ab not in started
                                started.add(ab)
                                c0 = col - 512 * bsub
                                cnt = e0 - s0
                                a_, b_ = s0 - qi0, e0 - qi0
                                if d == 1:
                                    oap = bank(ab)[0:128, :].rearrange("p (r i) -> p r i", r=4)[:, :, c0 // 4:(c0 + cnt) // 4]
                                    rap = pt4(pi)[0:kp, hh, ii, a_:b_].rearrange("p (i r) -> p r i", r=4)
                                elif d == 4:
                                    il0 = (c0 - r) // 4
                                    oap = bank(ab)[0:128, r * 128 + il0:r * 128 + il0 + cnt]
                                    rap = pt4(pi)[0:kp, hh, ii, a_:b_]
                                else:
                                    r4, a4 = r % 4, r // 4
                                    il0 = (c0 - r4) // 4
                                    oap = bank(ab)[0:128, r4 * 128 + il0:r4 * 128 + il0 + (cnt - 1) * 4 + 1:4]
                                    rap = pt4(pi)[0:kp, hh, ii, a_:b_]

                                def fpv(e, oap=oap, rap=rap, bi_=bi_, vtile=vtile, hh=hh, kp=kp, first=first):
                                    return e.matmul(oap, lhsT=vp4(bi_)[0:kp, vtile, hh, 0:128], rhs=rap,
                                                    start=first, stop=False, skip_group_check=True)
                                P.add("pe", fpv, r=kpt(pi, hh) + kvp(bi_) + [("vp", bi_, vtile // 4, hh)], w=[kb(ab)])
                            s0 = e0

                for gi, grp in enumerate(groups):
                    ng = len(grp)
                    (bi_, kp, nq, qoff) = grp[0][1]
                    d = grp[0][0][1]
                    sl = []
                    for (it, sig) in grp:
                        (_, _, r, ki0, _, qi0, _, _) = it
                        kt0 = ki0 * d + r
                        kt1 = kt0 + (kp - 1) * d + 1
                        qt0 = qi0 * d + r
                        qt1 = qt0 + (nq - 1) * d + 1
                        sl.append((slice(kt0, kt1, d), slice(qt0, qt1, d), kt0, kt1, qt0, qt1))
                    vi = 0
                    pi = ptrot.next()
                    xb0 = 4 + 2 * (gcount[0] % 2)
                    gcount[0] += 1

                    def fst(e, sl=sl, kp=kp, nq=nq, p=p, xb0=xb0):
                        ins = None
                        for ii, s_ in enumerate(sl):
                            for hh in range(2):
                                ins = e.matmul(bank(xb0 + hh)[0:kp, 256 * ii:256 * ii + nq],
                                               lhsT=k4[64 * hh:64 * (hh + 1), p, s_[0]],
                                               rhs=q4[64 * hh:64 * (hh + 1), p, s_[1]], start=True, stop=True,
                                               skip_group_check=True)
                        return ins
                    kkr = [k_ for s_ in sl for k_ in kqkv_rng("k", p, s_[2], s_[3]) + kqkv_rng("q", p, s_[4], s_[5])]
                    P.add("pe", fst, r=kkr, w=[kb(xb0), kb(xb0 + 1)])
                    ec = E_OFF[bi_] + qoff
                    xin = ps_t[:, xb0 * 512:(xb0 + 2) * 512].rearrange("p (h i q) -> p h i q", h=2, i=2)[0:kp, :, 0:ng, 0:nq]
                    P.add("act", lambda e, pi=pi, kp=kp, nq=nq, ng=ng, xin=xin: e.activation(
                        out=pt4(pi)[0:kp, :, 0:ng, 0:nq], in_=xin, func=AF.Exp),
                        r=[kb(xb0), kb(xb0 + 1)], w=kpt(pi, 0) + kpt(pi, 1))
                    P.add("dve", lambda e, pi=pi, kp=kp, nq=nq, ng=ng, ec=ec, p=p: e.tensor_tensor(
                        out=pt4(pi)[0:kp, :, 0:ng, 0:nq], in0=pt4(pi)[0:kp, :, 0:ng, 0:nq],
                        in1=E3[0:kp, 2 * p:2 * p + 2, ec:ec + nq].unsqueeze(2).broadcast_to([kp, 2, ng, nq]), op=ALU.mult),
                        r=kpt(pi, 0) + kpt(pi, 1) + ["E"], w=kpt(pi, 0) + kpt(pi, 1))
                    pq.append(("pv", lambda f=emit_pv, grp=grp, pi=pi: f(grp, pi, 0)))
                    last_of_branch = (gi + 1 == len(groups)) or (groups[gi + 1][0][1][0] != bi_)
                    if hf == 1 and p + 1 < 4 and last_of_branch:
                        pq.append(("misc", lambda p=p, b_=bi_: load_vp(p + 1, b_)))
                    pq_drain(LAG)
                    if gi >= 1 and deferred:
                        deferred.pop(0)()

                def mk_post(p=p, hf=hf):
                    steps = []
                    for hh in range(2):
                        def s_rd(hh=hh):
                            def frd(e, hh=hh):
                                e.activation(out=rbc_v[0:64, :], in_=osb(hh)[64:128, :], func=AF.Ln)
                                return e.activation(out=rbc_v[0:64, :], in_=rbc_v[0:64, :], func=AF.Exp, scale=-1.0)
                            P.add("act", frd, r=kosb(hh, 0) + kosb(hh, 1), w=K_RD)
                        steps.append(s_rd)
                        for j in range(2):
                            def s_y(hh=hh, j=j, p=p, hf=hf):
                                n = 2 * hf + j
                                P.add("dve", lambda e, hh=hh, j=j, n=n, p=p: e.tensor_tensor(
                                    out=q4[64 * hh:64 * (hh + 1), p, 512 * n:512 * (n + 1)],
                                    in0=osb(hh)[0:64, 512 * j:512 * (j + 1)], in1=rbc_v[0:64, 512 * j:512 * (j + 1)], op=ALU.mult),
                                    r=kosb(hh, j) + K_RD, w=[kqkv("q", p, n)])
                            steps.append(s_y)
                    return steps
                def block_end(mk_post=mk_post):
                    while deferred:
                        deferred.pop(0)()
                    for hh in range(2):
                        P.add("dve", lambda e, hh=hh: e.tensor_copy(
                            out=osb(hh)[:, 0:512].rearrange("p (i r) -> p i r", r=4),
                            in_=bank(2 * hh)[:, :].rearrange("p (r i) -> p i r", r=4)),
                              r=[kb(2 * hh)], w=kosb(hh, 0))
                        P.add("dve", lambda e, hh=hh: e.tensor_copy(
                            out=osb(hh)[:, 512:1024].rearrange("p (i r) -> p i r", r=4),
                            in_=bank(2 * hh + 1)[:, :].rearrange("p (r i) -> p i r", r=4)),
                              r=[kb(2 * hh + 1)], w=kosb(hh, 1))
                    deferred.extend(mk_post())
                pq.append(("misc", block_end))
        pq_drain(0)
        while deferred:
            deferred.pop(0)()

        P.enabled = "B" in phases
        si_w0, wv0 = load_w(w_out_d[l, 256:512, :], 2, 1024)
        si_w1, wv1 = load_w(w_out_d[l, 512:768, :], 2, 1024)
        for kc4 in range(4):
            si_w, wv = (si_w0, wv0) if kc4 < 2 else (si_w1, wv1)
            gc = PO_GOUT + 8 * l + 2 + kc4
            P.add("dve", lambda e, kc=kc4 % 2, gc=gc, wv=wv: e.tensor_scalar(
                out=wv[:, kc, :], in0=wv[:, kc, :], scalar1=pcol(gc), scalar2=None, op0=ALU.mult),
                r=[kw(si_w), "prm"], w=[kw(si_w)])
        for n in range(4):
            bi = brot.next()
            for pp in range(4):
                qi = sqrot.next()
                P.add("act", lambda e, n=n, pp=pp, qi=qi: e.activation(out=sq(qi), in_=q4[:, pp, 512 * n:512 * (n + 1)],
                                                                       func=AF.Square),
                      r=[kqkv("q", pp, n)], w=[ksq(qi)])
                P.add("pe", lambda e, pp=pp, qi=qi, bi=bi: e.matmul(bank(bi), lhsT=onesB_t[:], rhs=sq(qi),
                                                                    start=(pp == 0), stop=(pp == 3)),
                      r=[ksq(qi), "onesB"], w=[kb(bi)])
            rs = rstd_from_ss(bi, 1.0 / 512)
            for m in range(8):
                b2 = brot.next()

                def fo(e, m=m, n=n, b2=b2, wv0=wv0, wv1=wv1):
                    ins = None
                    for pp in range(4):
                        wv = wv0 if pp < 2 else wv1
                        ins = e.matmul(bank(b2), lhsT=wv[:, pp % 2, m * 128:(m + 1) * 128],
                                       rhs=q4[:, pp, 512 * n:512 * (n + 1)], start=(pp == 0), stop=(pp == 3))
                    return ins
                P.add("pe", fo, r=[kqkv("q", pp, n) for pp in range(4)] + [kw(si_w0), kw(si_w1)], w=[kb(b2)])
                st = (rs + 1 + (m % 3)) % NSM
                P.add("dve", lambda e, b2=b2, rs=rs, st=st: e.tensor_tensor(out=sm(st), in0=bank(b2), in1=sm(rs), op=ALU.mult),
                      r=[kb(b2), ksm(rs)], w=[ksm(st)])
                P.add("dve", lambda e, m=m, n=n, st=st: e.tensor_tensor(
                    out=xT[:, m, 512 * n:512 * (n + 1)], in0=xT[:, m, 512 * n:512 * (n + 1)], in1=sm(st), op=ALU.add),
                    r=[ksm(st), kx(m, n)], w=[kx(m, n)])

        P.enabled = "F" in phases
        act3 = qkv_t[:, 0:22 * 1024].rearrange("p (j t) -> p j t", j=22)
        def kact(j, n2=None):
            return KR("Q", j * 2048, 2048) if n2 is None else KR("Q", j * 2048 + n2 * 1024, 1024)
        h2 = hT_t[:, 0:8 * 1024].rearrange("p (c t) -> p c t", c=8)
        def kh2(c, n2): return KR("H", (c * 1024 + n2 * 512) * 2, 1024)
        ft_v = hT_t[:, 8 * 1024:16 * 1024].bitcast(F32)
        ft2_v = yac_t[:].bitcast(F32)
        ftiles = [ft_v[:, 1024 * i:1024 * (i + 1)] for i in range(4)] + [ft2_v[:, 1024 * i:1024 * (i + 1)] for i in range(2)]
        def kft(i): return KR("H", 16384 + 4096 * i, 4096) if i < 4 else KR("Y", 4096 * (i - 4), 4096)

        def h2rhs(kc, n2):
            return h2[:, kc, 512 * n2:512 * (n2 + 1)]
        brot = Rot(range(8))
        for hf in range(2):
            if hf == 0:
                rmsnorm_tiles(PO_GFFN + 8 * l, [0, 1],
                              lambda c, n: (h2[:, c, 512 * (n % 2):512 * (n % 2 + 1)], kh2(c, n % 2)))
            ftrot = Rot([0, 1, 2, 3, 4, 5])
            dn = {}
            def load_dn(m, part):
                dn[(m, part)] = load_w(w_dn_d[l, 1408 * part:1408 * (part + 1), m * 128:(m + 1) * 128], 11, 128)
            upw = {}
            def load_up(jj):
                upw[jj] = (load_w(w_up_d[l, :, 256 * jj:256 * (jj + 1)], 8, 256),
                           load_w(w_up_d[l, :, DFF + 256 * jj:DFF + 256 * (jj + 1)], 8, 256))
            load_up(0)
            for jj in range(11):
                if jj + 1 < 11:
                    load_up(jj + 1)
                else:
                    load_dn(0, 0)
                    load_dn(1, 0)
                    load_dn(0, 1)
                (si_g, wg), (si_v, wv_) = upw[jj]
                for j2 in range(2):
                    j = 2 * jj + j2
                    R = {}
                    for gv, (si_w, wv) in enumerate(((si_g, wg), (si_v, wv_))):
                        ch = j + 22 * gv
                        fi = ftrot.next()
                        R[gv] = fi
                        Rt = ftiles[fi]
                        cw = PO_CVF + (l * 3) * 44 + ch
                        b0 = brot.next()
                        b1 = brot.next()
                        assert b0 % 2 == 0 and b1 == b0 + 1
                        ps2 = ps_t[:, 512 * b0:512 * (b0 + 2)]
                        for n2, bi in ((0, b0), (1, b1)):
                            P.add("pe", mm8(wv, j2 * 128, h2rhs, n2, bi),
                                  r=[k_ for kc in range(8) for k_ in kh2(kc, n2)] + [kw(si_w)], w=[kb(bi)])
                        P.add("act", lambda e, ps2=ps2, Rt=Rt, cw=cw: e.activation(
                            out=Rt[:, :], in_=ps2, func=AF.Identity, scale=pcol(cw + 88)),
                            r=[kb(b0), kb(b1), "prm"], w=kft(fi))

                        def ftap(e, Rt=Rt, ps2=ps2, cw=cw, ch=ch, hf=hf):
                            w1, w0 = pcol(cw + 44), pcol(cw)
                            stt = e.scalar_tensor_tensor
                            stt(out=Rt[:, 1:1024], in0=ps2[:, 0:1023], scalar=w1, in1=Rt[:, 1:1024], op0=ALU.mult, op1=ALU.add)
                            ins = stt(out=Rt[:, 2:1024], in0=ps2[:, 0:1022], scalar=w0, in1=Rt[:, 2:1024], op0=ALU.mult, op1=ALU.add)
                            hl = halo_t[:, 2 * ch:2 * ch + 2]
                            if hf == 1:
                                stt(out=Rt[:, 0:1], in0=hl[:, 1:2], scalar=w1, in1=Rt[:, 0:1], op0=ALU.mult, op1=ALU.add)
                                ins = stt(out=Rt[:, 0:2], in0=hl[:, 0:2], scalar=w0, in1=Rt[:, 0:2], op0=ALU.mult, op1=ALU.add)
                            else:
                                ins = e.tensor_copy(out=hl, in_=ps2[:, 1022:1024])
                            return ins
                        P.add("dve", ftap, r=[kb(b0), kb(b1), "prm", ("halo", ch)] + kft(fi), w=kft(fi) + [("halo", ch)])
                    fT = ftrot.next()
                    Tt = ftiles[fT]
                    Rg, Rv = ftiles[R[0]], ftiles[R[1]]
                    P.add("act", lambda e, Tt=Tt, Rg=Rg: e.activation(out=Tt, in_=Rg, func=AF.Silu),
                          r=kft(R[0]), w=kft(fT))
                    P.add("pool", lambda e, Tt=Tt, Rv=Rv, j=j: e.tensor_tensor(out=act3[:, j, :], in0=Tt, in1=Rv, op=ALU.mult),
                          r=kft(fT) + kft(R[1]), w=kact(j))
            if hf == 0:
                rmsnorm_tiles(PO_GFFN + 8 * l, [2, 3],
                              lambda c, n: (h2[:, c, 512 * (n % 2):512 * (n % 2 + 1)], kh2(c, n % 2)))
            load_dn(1, 1)

            def fd(e, n2, bi, wv, j0):
                ins = None
                for j in range(j0, j0 + 11):
                    ins = e.matmul(bank(bi), lhsT=wv[:, j - j0, :], rhs=act3[:, j, 512 * n2:512 * (n2 + 1)],
                                   start=(j == 0), stop=(j == 21))
                return ins
            for mp in range(4):
                ms = (2 * mp, 2 * mp + 1)
                bis = {(m, n2): brot.next() for m in ms for n2 in range(2)}
                for m in ms:
                    sd0, wd0 = dn[(m, 0)]
                    for n2 in range(2):
                        P.add("pe", lambda e, n2=n2, bi=bis[(m, n2)], wv=wd0: fd(e, n2, bi, wv, 0),
                              r=[k_ for j in range(11) for k_ in kact(j, n2)] + [kw(sd0)], w=[kb(bis[(m, n2)])])
                if mp + 1 < 4:
                    load_dn(2 * mp + 2, 0)
                    load_dn(2 * mp + 3, 0)
                for m in ms:
                    sd1, wd1 = dn[(m, 1)]
                    for n2 in range(2):
                        P.add("pe", lambda e, n2=n2, bi=bis[(m, n2)], wv=wd1: fd(e, n2, bi, wv, 11),
                              r=[k_ for j in range(11, 22) for k_ in kact(j, n2)] + [kw(sd1)], w=[kb(bis[(m, n2)])])
                        n = 2 * hf + n2
                        P.add("dve", lambda e, m=m, n=n, bi=bis[(m, n2)]: e.tensor_tensor(
                            out=xT[:, m, 512 * n:512 * (n + 1)], in0=xT[:, m, 512 * n:512 * (n + 1)], in1=bank(bi), op=ALU.add),
                            r=[kb(bis[(m, n2)]), kx(m, n)], w=[kx(m, n)])
                if mp + 1 < 4:
                    load_dn(2 * mp + 2, 1)
                    load_dn(2 * mp + 3, 1)

    P.enabled = True
    if dump:
        dq0_d = nc.dram_tensor("dbg_q0", [128, 12 * S], BF16, kind="ExternalOutput").ap()
        P.add("sp", lambda e: e.dma_start(out=dq0_d[:, :], in_=qkv_t[:, :]), r=KR("Q", 0, 49152), w=[("out", 3)], dma=True)
    fin3 = stgF.rearrange("p (c t) -> p c t", c=8)
    ostg = qkv_t[:].bitcast(F32)[:, 0:8 * 1024].rearrange("p (j d) -> p j d", j=8)
    def kfin(c, t0, nt): return KR("H", (c * 1024 + t0) * 4, nt * 4)
    for half in range(2):
        if final_norm:
            rmsnorm_tiles(PO_GFIN, [2 * half, 2 * half + 1],
                          lambda c, n: (fin3[:, c, 512 * (n % 2):512 * (n % 2 + 1)], kfin(c, 512 * (n % 2), 512)))
        for jt in range(8):
            for cg in range(2):
                bi = brot.next()

                def ftr(e, jt=jt, cg=cg, bi=bi, half=half):
                    ins = None
                    for i in range(4):
                        c = cg * 4 + i
                        if final_norm:
                            src = fin3[:, c, 128 * jt:128 * (jt + 1)]
                        else:
                            src = xT[:, c, 1024 * half + 128 * jt:1024 * half + 128 * (jt + 1)]
                        ins = e.transpose(bank(bi)[:, 128 * i:128 * (i + 1)], src, identF_t[:])
                    return ins
                if final_norm:
                    rr = [k_ for i in range(4) for k_ in kfin(cg * 4 + i, 128 * jt, 128)]
                else:
                    rr = [kx(cg * 4 + i, 2 * half + jt // 4) for i in range(4)]
                P.add("pe", ftr, r=rr + ["identF"], w=[kb(bi)])
                dst = ostg[:, jt, 512 * cg:512 * (cg + 1)]
                kd = KR("Q", (jt * 1024 + 512 * cg) * 4, 2048)
                if evrot.next() == "act":
                    P.add("act", lambda e, dst=dst, bi=bi: e.copy(out=dst, in_=bank(bi)), r=[kb(bi)], w=kd)
                else:
                    P.add("dve", lambda e, dst=dst, bi=bi: e.tensor_copy(out=dst, in_=bank(bi)), r=[kb(bi)], w=kd)
        dsto = out_d[half * 1024:(half + 1) * 1024, :].rearrange("(j p) d -> p j d", p=128)
        P.add("sp", lambda e, dsto=dsto: e.dma_start(out=dsto, in_=ostg), r=KR("Q", 0, 32768),
              w=[("out", half)], dma=True)
    if dump:
        dh_d = nc.dram_tensor("dbg_h", [128, 8 * S], BF16, kind="ExternalOutput").ap()
        P.add("sp", lambda e: e.dma_start(out=dh_d[:, :], in_=hT_t[:, :]), r=KH_ALL, w=[("out", 2)], dma=True)
        P.add("sp", lambda e: None, r=[("out", 0), ("out", 1), ("out", 2), ("out", 3)])
    else:
        P.add("sp", lambda e: None, r=[("out", 0), ("out", 1)])

    P.emit(nc, stack)
    stack.close()
    return nc


def _t5_bucket_np(dist):
    num_buckets, max_distance = 32, 2048
    max_exact = num_buckets // 2
    d_f = np.maximum(dist, 1).astype(np.float32)
    large = max_exact + (np.log(d_f / max_exact) / math.log(max_distance / max_exact)
                         * (num_buckets - max_exact)).astype(np.int32)
    large = np.minimum(large, num_buckets - 1)
    return np.where(dist < max_exact, dist, large)


def _fm(a):
    a = np.asarray(a, np.float32)
    lead = a.shape[:-1]
    c = a.shape[-1] // 128
    a = a.reshape(lead + (c, 128))
    a = np.moveaxis(a, -1, 0)
    return np.ascontiguousarray(a).reshape(128, -1)


def _host_tables(inp):
    prm = np.zeros((128, NPRM), np.float32)
    prm[:, PO_GMIX:PO_GMIX + 32] = _fm(inp["norm_mix_g"])
    prm[:, PO_GFFN:PO_GFFN + 32] = _fm(inp["norm_ffn_g"])
    prm[:, PO_GOUT:PO_GOUT + 32] = _fm(inp["out_norm_g"])
    prm[:, PO_GFIN:PO_GFIN + 8] = _fm(inp["final_g"])
    prm[:, PO_CVA:PO_CVA + 24] = _fm(inp["conv_a_w"])
    prm[:, PO_CVC:PO_CVC + 248] = _fm(inp["conv_c_w"])
    prm[:, PO_CVCB:PO_CVCB + 8] = _fm(inp["conv_c_b"])
    prm[:, PO_LNG:PO_LNG + 8] = _fm(inp["ln_c_g"])
    prm[:, PO_LNB:PO_LNB + 8] = _fm(inp["ln_c_b"])
    prm[:, PO_CVF:PO_CVF + 528] = _fm(inp["conv_f_w"])
    rb = np.asarray(inp["rel_bias"], np.float32)
    jk = np.arange(128)[:, None]
    bg = np.zeros((128, 8, E_W), np.float32)
    mk = np.zeros((128, E_W), np.float32)
    for bi, d in enumerate((1, 4, 16)):
        w = 256 if d != 16 else 128
        rel = np.arange(w)[None, :] - jk
        idx = _t5_bucket_np(np.maximum(rel, 0) * d)
        bg[:, :, E_OFF[bi]:E_OFF[bi] + w] = np.transpose(rb[idx], (0, 2, 1))
        mk[:, E_OFF[bi]:E_OFF[bi] + w] = ((rel >= 0) & (rel <= 128)).astype(np.float32)
    return prm, bg.reshape(128, 8 * E_W), mk


_CACHE = {}


def kernel(**inputs):
    inp = {k: np.asarray(v) for k, v in inputs.items()}
    x = np.ascontiguousarray(inp["x"], dtype=np.float32)
    prm, bg, mk = _host_tables(inp)
    if "nc" not in _CACHE:
        _CACHE["nc"] = build_program(DEPTH)
    nc = _CACHE["nc"]
    shared = {
        "w_in": np.ascontiguousarray(inp["w_in"], dtype=np.float32),
        "w_out": np.ascontiguousarray(inp["w_out"], dtype=np.float32),
        "w_up": np.ascontiguousarray(inp["w_up"], dtype=np.float32),
        "w_down": np.ascontiguousarray(inp["w_down"], dtype=np.float32),
        "prm": prm, "biasg": bg, "mask01": mk,
    }
    in_maps = [dict(shared, x=x[b]) for b in range(NCORES)]
    res = run_bass_kernel_spmd(nc, in_maps, core_ids=list(range(NCORES)))
    return np.stack([np.asarray(r["out"], dtype=np.float32) for r in res.results], axis=0)
```

```python
import math
from contextlib import ExitStack

import numpy as np
import concourse.bass as bass
import concourse.mybir as mybir
from concourse.bass_utils import run_bass_kernel_spmd

F32 = mybir.dt.float32
BF16 = mybir.dt.bfloat16
ALU = mybir.AluOpType
AF = mybir.ActivationFunctionType

S = 2048
D = 1024
DEPTH = 4
IN_COLS = 2816
DFF = 2816
EPS = 1e-6
NCORES = 8

PO_GMIX = 0
PO_GFFN = 32
PO_GOUT = 64
PO_GFIN = 96
PO_CVA = 104
PO_CVC = 128
PO_CVCB = 376
PO_LNG = 384
PO_LNB = 392
PO_HLNB = 400
PO_CVF = 408
NPRM = 408 + 528


class _Op:
    __slots__ = ("eng", "fn", "reads", "writes", "dma", "waits", "sig", "need_sig", "deps")


class Prog:
    ENGS = ("pe", "act", "dve", "pool", "sp")

    def __init__(self):
        self.ops = []
        self.enabled = True

    def add(self, eng, fn, r=(), w=(), dma=False):
        if not self.enabled:
            return None
        o = _Op()
        o.eng, o.fn, o.reads, o.writes, o.dma = eng, fn, tuple(r), tuple(w), dma
        o.waits, o.sig, o.need_sig, o.deps = [], None, False, ()
        self.ops.append(o)
        return o

    def resolve(self):
        ops = self.ops
        last_w = {}
        rd_eng = {}
        rd_dma = {}
        for i, o in enumerate(ops):
            deps = set()
            for k in o.reads:
                j = last_w.get(k)
                if j is not None:
                    deps.add(j)
            for k in o.writes:
                j = last_w.get(k)
                if j is not None:
                    deps.add(j)
                d = rd_eng.get(k)
                if d:
                    deps.update(d.values())
                d2 = rd_dma.get(k)
                if d2:
                    deps.update(d2)
            deps.discard(i)
            o.deps = tuple(j for j in deps if ops[j].dma or ops[j].eng != o.eng)
            for j in o.deps:
                ops[j].need_sig = True
            for k in o.writes:
                last_w[k] = i
                rd_eng.pop(k, None)
                rd_dma.pop(k, None)
            wset = set(o.writes)
            for k in o.reads:
                if k in wset:
                    continue
                if o.dma:
                    rd_dma.setdefault(k, []).append(i)
                else:
                    rd_eng.setdefault(k, {})[o.eng] = i

    def emit(self, nc, stack, ndma=12):
        self.resolve()
        ops = self.ops
        sems = {e: stack.enter_context(nc.semaphore("sem_" + e)) for e in self.ENGS}
        dsems = {q: [stack.enter_context(nc.semaphore("dsem_%s_%d" % (q, i))) for i in range(ndma)]
                 for q in ("sp", "pool")}
        cnt = {e: 0 for e in self.ENGS}
        dcnt = {"sp": 0, "pool": 0}
        pre = {}
        for i, o in enumerate(ops):
            if o.dma:
                k = dcnt[o.eng]
                dcnt[o.eng] += 1
                sem = dsems[o.eng][k % ndma]
                o.sig = (sem, 16 * (k // ndma + 1))
                if k >= ndma:
                    pre[i] = (sem, 16 * (k // ndma))
            elif o.need_sig:
                cnt[o.eng] += 1
                o.sig = (sems[o.eng], cnt[o.eng])
        waited = {e: {} for e in self.ENGS}
        for i, o in enumerate(ops):
            need = {}
            if i in pre:
                s, v = pre[i]
                need[id(s)] = (s, v)
            for j in o.deps:
                s, v = ops[j].sig
                cur = need.get(id(s))
                if cur is None or cur[1] < v:
                    need[id(s)] = (s, v)
            wl = []
            wd = waited[o.eng]
            for sid, (s, v) in need.items():
                if wd.get(sid, 0) >= v:
                    continue
                wd[sid] = v
                wl.append((s, v))
            o.waits = wl
        by_eng = {e: [o for o in ops if o.eng == e] for e in self.ENGS}

        def run(handle, name):
            for o in by_eng[name]:
                for s, v in o.waits:
                    handle.wait_ge(s, v)
                inst = o.fn(handle)
                if o.sig is not None:
                    assert inst is not None
                    inst.then_inc(o.sig[0], 16 if o.dma else 1)

        with nc.Block() as block:
            @block.tensor
            def _(e):
                run(e, "pe")

            @block.scalar
            def _(e):
                run(e, "act")

            @block.vector
            def _(e):
                run(e, "dve")

            @block.gpsimd
            def _(e):
                run(e, "pool")

            @block.sync
            def _(e):
                run(e, "sp")


class Rot:
    def __init__(self, items):
        self.items = list(items)
        self.i = 0

    def next(self):
        v = self.items[self.i % len(self.items)]
        self.i += 1
        return v


def attn_items(hf):
    items = []
    for bi, d in enumerate((1, 4, 16)):
        L = S // d
        Lh = 1024 // d
        qlo, qhi = Lh * hf, Lh * (hf + 1)
        for r in range(d):
            for kt in range(L // 128):
                a = max(128 * kt, qlo)
                b = min(128 * kt + 256, qhi, L)
                if b <= a:
                    continue
                kp = min(128, b - 128 * kt)
                items.append((bi, d, r, 128 * kt, kp, a, b - a, a - 128 * kt))
    return items


E_OFF = (0, 256, 512)
E_W = 640


def build_program(n_layers, final_norm=True, phases="NACQTBF", dump=False):
    nc = bass.Bass("TRN2", target_bir_lowering=False)
    P = Prog()
    stack = ExitStack()

    x_d = nc.dram_tensor("x", [S, D], F32, kind="ExternalInput").ap()
    w_in_d = nc.dram_tensor("w_in", [DEPTH, D, IN_COLS], F32, kind="ExternalInput").ap()
    w_out_d = nc.dram_tensor("w_out", [DEPTH, D, D], F32, kind="ExternalInput").ap()
    w_up_d = nc.dram_tensor("w_up", [DEPTH, D, 2 * DFF], F32, kind="ExternalInput").ap()
    w_dn_d = nc.dram_tensor("w_down", [DEPTH, DFF, D], F32, kind="ExternalInput").ap()
    prm_d = nc.dram_tensor("prm", [128, NPRM], F32, kind="ExternalInput").ap()
    bg_d = nc.dram_tensor("biasg", [128, 8 * E_W], F32, kind="ExternalInput").ap()
    mk_d = nc.dram_tensor("mask01", [128, E_W], F32, kind="ExternalInput").ap()
    out_d = nc.dram_tensor("out", [S, D], F32, kind="ExternalOutput").ap()
    vscr_d = nc.dram_tensor("vscr", [S, 512], BF16, kind="Internal").ap()

    sb = lambda name, shape, dt: stack.enter_context(nc.sbuf_tensor(name, shape, dt))
    xT_t = sb("xT", [128, 8 * S], F32)
    hT_t = sb("hT", [128, 8 * S], BF16)
    qkv_t = sb("qkv", [128, 12 * S], BF16)
    yac_t = sb("yac", [128, 2 * S], BF16)
    E_t = sb("E", [128, 8 * E_W], BF16)
    WSLOT = 2048
    NW = 5
    wr_t = sb("wring", [128, NW * WSLOT], BF16)
    NSM = 6
    sm_t = sb("sm", [128, NSM * 512], F32)
    NSQ = 3
    sq_t = sb("sq", [128, NSQ * 512], BF16)
    prm_t = sb("prm_sb", [128, NPRM], F32)
    identF_t = sb("identF", [128, 128], F32)
    identB_t = sb("identB", [128, 128], BF16)
    onesB_t = sb("onesB", [128, 128], BF16)
    onesF_t = sb("onesF", [128, 64], F32)
    NDG = 6
    dg_t = sb("diag", [128, NDG * 128], BF16)
    halo_t = sb("halo", [128, 44 * 2], F32)
    ps_t = stack.enter_context(nc.psum_tensor("ps", [128, 8 * 512], F32))

    xT = xT_t[:].rearrange("p (c t) -> p c t", c=8)
    hT = hT_t[:].rearrange("p (c t) -> p c t", c=8)
    q4 = qkv_t[:, 0:4 * S].rearrange("p (c t) -> p c t", c=4)
    k4 = qkv_t[:, 4 * S:8 * S].rearrange("p (c t) -> p c t", c=4)
    v4 = qkv_t[:, 8 * S:12 * S].rearrange("p (c t) -> p c t", c=4)
    yac = yac_t[:].rearrange("p (c t) -> p c t", c=2)
    pbuf_t = qkv_t[:, 4 * S:4 * S + 2 * (2 + S)].bitcast(F32)
    E3 = E_t[:].rearrange("p (h w) -> p h w", h=8)
    prm = prm_t
    bank = lambda i: ps_t[:, 512 * i:512 * (i + 1)]

    def pcol(off):
        return prm[:, off:off + 1]

    GB = 1024

    def KR(region, b0, nb):
        return [(region, g) for g in range(b0 // GB, (b0 + nb - 1) // GB + 1)]

    def kx(c, n): return ("x", c, n)
    def kh(c, n): return ("H", c * 4 + n)
    KH_ALL = KR("H", 0, 32768)
    QOFF = {"q": 0, "k": 16384, "v": 32768}
    def kqkv(which, p, n): return ("Q", (QOFF[which] + (p * 2048 + n * 512) * 2) // GB)
    def kqkv_rng(which, p, t0, t1): return KR("Q", QOFF[which] + (p * 2048 + t0) * 2, (t1 - t0) * 2)
    def kb(i): return ("ps", i)
    def kyac(c, n): return ("Y", c * 4 + n)

    def kyac_all(): return KR("Y", 0, 8192)
    vt_t = sb("vt", [128, 4 * 260], BF16)

    P.add("sp", lambda e: e.dma_start(out=prm[:, :], in_=prm_d[:, :]), w=["prm"], dma=True)

    def f_ident(e):
        e.memset(identF_t[:], 0.0)
        return e.affine_select(out=identF_t[:], in_=identF_t[:], pattern=[[-1, 128]], compare_op=ALU.not_equal,
                               fill=1.0, base=0, channel_multiplier=1)
    P.add("pool", f_ident, w=["identF"])
    P.add("dve", lambda e: e.tensor_copy(out=identB_t[:], in_=identF_t[:]), r=["identF"], w=["identB"])
    P.add("dve", lambda e: e.memset(onesB_t[:], 1.0), w=["onesB"])
    P.add("dve", lambda e: e.memset(onesF_t[:], 1.0), w=["onesF"])
    P.add("dve", lambda e: e.memset(vt_t[:], 1.0), w=[("vt", i) for i in range(4)])
    P.add("dve", lambda e: e.tensor_scalar(out=prm[:, PO_HLNB:PO_HLNB + 8], in0=prm[:, PO_LNB:PO_LNB + 8],
                                           scalar1=0.5, scalar2=None, op0=ALU.mult), r=["prm"], w=["prm"])
    stgF = hT_t[:].bitcast(F32)
    stgQ = qkv_t[:].bitcast(F32)
    stg3 = stgF.rearrange("p (j d) -> p j d", j=8)
    stg3b = stgQ[:, 0:8192].rearrange("p (j d) -> p j d", j=8)
    P.add("sp", lambda e: e.dma_start(out=stg3, in_=x_d[0:1024, :].rearrange("(j p) d -> p j d", p=128)),
          w=KH_ALL, dma=True)
    bgv = stgQ[:, 0:8 * E_W]
    mkv = stgQ[:, 8 * E_W:9 * E_W]
    K_BG = KR("Q", 0, 8 * E_W * 4)
    K_MK = KR("Q", 8 * E_W * 4, E_W * 4)
    P.add("sp", lambda e: e.dma_start(out=bgv, in_=bg_d[:, :]), w=K_BG, dma=True)
    P.add("sp", lambda e: e.dma_start(out=mkv, in_=mk_d[:, :]), w=K_MK, dma=True)
    P.add("act", lambda e: e.activation(out=bgv, in_=bgv, func=AF.Exp), r=K_BG, w=K_BG)

    def f_E(e):
        ins = None
        bg3 = bgv.rearrange("p (h w) -> p h w", h=8)
        for h in range(8):
            ins = e.tensor_tensor(out=E3[:, h, :], in0=bg3[:, h, :], in1=mkv, op=ALU.mult)
        return ins
    P.add("dve", f_E, r=K_BG + K_MK, w=["E"])

    brot = Rot(range(8))
    evrot = Rot(["act", "dve"])
    for half in range(2):
        stg_h = stg3 if half == 0 else stg3b
        reg_h = "H" if half == 0 else "Q"
        if half == 1:
            src = x_d[1024:2048, :].rearrange("(j p) d -> p j d", p=128)
            P.add("sp", lambda e, src=src: e.dma_start(out=stg3b, in_=src), w=KR("Q", 0, 32768), dma=True)
        for c in range(8):
            for g in range(2):
                bi = brot.next()

                def f_tr(e, c=c, g=g, bi=bi, stg_h=stg_h):
                    ins = None
                    for i in range(4):
                        ins = e.transpose(bank(bi)[:, 128 * i:128 * (i + 1)], stg_h[:, g * 4 + i, c * 128:(c + 1) * 128],
                                          identF_t[:])
                    return ins
                P.add("pe", f_tr, r=KR(reg_h, g * 4 * 4096, 4 * 4096) + ["identF"], w=[kb(bi)])
                n = half * 2 + g
                dst = xT[:, c, n * 512:(n + 1) * 512]
                if evrot.next() == "act":
                    P.add("act", lambda e, dst=dst, bi=bi: e.copy(out=dst, in_=bank(bi)), r=[kb(bi)], w=[kx(c, n)])
                else:
                    P.add("dve", lambda e, dst=dst, bi=bi: e.tensor_copy(out=dst, in_=bank(bi)), r=[kb(bi)], w=[kx(c, n)])

    smrot = Rot(range(NSM))
    sqrot = Rot(range(NSQ))
    wrot = Rot(range(NW))
    dgrot = Rot(range(NDG))

    def sm(i): return sm_t[:, 512 * i:512 * (i + 1)]
    def ksm(i): return ("sm", i)
    def sq(i): return sq_t[:, 512 * i:512 * (i + 1)]
    def ksq(i): return ("sq", i)
    def wslot(i): return wr_t[:, WSLOT * i:WSLOT * (i + 1)]
    def kw(i): return ("w", i)

    def load_w(src_ap, shape_kc, ncols):
        si = wrot.next()
        assert shape_kc * ncols <= WSLOT
        dst = wslot(si)[:, 0:shape_kc * ncols].rearrange("p (k n) -> p k n", k=shape_kc)
        P.add("pool", lambda e: e.dma_start(out=dst, in_=src_ap.rearrange("(k p) n -> p k n", p=128)),
              w=[kw(si)], dma=True)
        return si, dst

    def rstd_from_ss(bi, scale):
        si = smrot.next()

        def f(e):
            e.activation(out=sm(si), in_=bank(bi), func=AF.Ln, bias=EPS, scale=scale)
            return e.activation(out=sm(si), in_=sm(si), func=AF.Exp, scale=-0.5)
        P.add("act", f, r=[kb(bi)], w=[ksm(si)])
        return si

    def rmsnorm_tiles(gcol, token_tiles, dst_of):
        for n in token_tiles:
            bi = brot.next()
            for c in range(8):
                qi = sqrot.next()
                P.add("act", lambda e, c=c, n=n, qi=qi: e.activation(out=sq(qi), in_=xT[:, c, n * 512:(n + 1) * 512],
                                                                     func=AF.Square),
                      r=[kx(c, n)], w=[ksq(qi)])
                P.add("pe", lambda e, c=c, qi=qi, bi=bi: e.matmul(bank(bi), lhsT=onesB_t[:], rhs=sq(qi),
                                                                  start=(c == 0), stop=(c == 7)),
                      r=[ksq(qi), "onesB"], w=[kb(bi)])
            si = rstd_from_ss(bi, 1.0 / D)
            for c in range(8):
                dap, dkeys = dst_of(c, n)
                P.add("dve", lambda e, c=c, n=n, si=si, dap=dap: e.scalar_tensor_tensor(
                    out=dap, in0=xT[:, c, n * 512:(n + 1) * 512], scalar=pcol(gcol + c), in1=sm(si),
                    op0=ALU.mult, op1=ALU.mult), r=[kx(c, n), ksm(si), "prm"], w=dkeys)

    def hrhs(kc, n):
        return hT[:, kc, n * 512:(n + 1) * 512]

    def mm8(wv, col0, rhs_fn, n, bi):
        def f(e):
            ins = None
            for kc in range(8):
                ins = e.matmul(bank(bi), lhsT=wv[:, kc, col0:col0 + 128], rhs=rhs_fn(kc, n),
                               start=(kc == 0), stop=(kc == 7))
            return ins
        return f

    for l in range(n_layers):
        P.enabled = "N" in phases
        rmsnorm_tiles(PO_GMIX + 8 * l, range(4), lambda c, n: (hT[:, c, n * 512:(n + 1) * 512], [kh(c, n)]))
        KHN = lambda n: [kh(kc, n) for kc in range(8)]

        P.enabled = "A" in phases
        sA = {}
        for name, col in (("h", 0), ("b", 256), ("c", 512)):
            sA[name] = load_w(w_in_d[l, :, col:col + 256], 8, 256)
        brot = Rot(range(8))
        pendA = []
        def kpb(t0, nt): return KR("Q", 16384 + t0 * 4, nt * 4)
        P.add("dve", lambda e: e.memset(pbuf_t[:, 0:2], 0.0), w=kpb(0, 2))
        for cc in range(2):
            for n in range(4):
                si_w, wv = sA["h"]
                bi = brot.next()
                P.add("pe", mm8(wv, cc * 128, hrhs, n, bi), r=KHN(n) + [kw(si_w)], w=[kb(bi)])
                si = smrot.next()
                P.add("act", lambda e, si=si, bi=bi: e.copy(out=sm(si), in_=bank(bi)), r=[kb(bi)], w=[ksm(si)])
                si_c, wc = sA["c"]
                bi2 = brot.next()
                P.add("pe", mm8(wc, cc * 128, hrhs, n, bi2), r=KHN(n) + [kw(si_c)], w=[kb(bi2)])
                P.add("dve", lambda e, n=n, bi2=bi2, si=si: e.tensor_tensor(
                    out=pbuf_t[:, 2 + 512 * n:2 + 512 * (n + 1)], in0=bank(bi2), in1=sm(si), op=ALU.mult),
                    r=[kb(bi2), ksm(si)], w=kpb(2 + 512 * n, 512))
            si_b, wb = sA["b"]
            for n in range(4):
                bi = brot.next()
                P.add("pe", mm8(wb, cc * 128, hrhs, n, bi), r=KHN(n) + [kw(si_b)], w=[kb(bi)])
                si = smrot.next()
                cw = PO_CVA + (l * 3) * 2 + cc
                pk = kpb(512 * n, 514)
                P.add("act", lambda e, n=n, si=si, cw=cw: e.activation(
                    out=sm(si), in_=pbuf_t[:, 2 + 512 * n:2 + 512 * (n + 1)], func=AF.Identity, scale=pcol(cw + 4)),
                    r=pk + ["prm"], w=[ksm(si)])

                def f4(e, n=n, si=si, cw=cw):
                    e.scalar_tensor_tensor(out=sm(si), in0=pbuf_t[:, 1 + 512 * n:1 + 512 * (n + 1)], scalar=pcol(cw + 2),
                                           in1=sm(si), op0=ALU.mult, op1=ALU.add)
                    return e.scalar_tensor_tensor(out=sm(si), in0=pbuf_t[:, 512 * n:512 * (n + 1)], scalar=pcol(cw),
                                                  in1=sm(si), op0=ALU.mult, op1=ALU.add)
                P.add("dve", f4, r=pk + [ksm(si), "prm"], w=[ksm(si)])
                P.add("dve", lambda e, n=n, si=si, bi=bi, cc=cc: e.tensor_tensor(
                    out=yac[:, cc, 512 * n:512 * (n + 1)], in0=bank(bi), in1=sm(si), op=ALU.mult),
                    r=[kb(bi), ksm(si)], w=[kyac(cc, n)])
                pendA.append((cc, n))

        def finish_group(ss_banks, scale_ss, row0, post_scale, mid=None):
            rs = {}
            for n in range(4):
                rs[n] = rstd_from_ss(ss_banks[n], scale_ss)
            for n in range(4):
                for cc in range(2):
                    P.add("dve", lambda e, n=n, cc=cc: e.scalar_tensor_tensor(
                        out=yac[:, cc, 512 * n:512 * (n + 1)], in0=yac[:, cc, 512 * n:512 * (n + 1)], scalar=post_scale,
                        in1=sm(rs[n]), op0=ALU.mult, op1=ALU.mult), r=[kyac(cc, n), ksm(rs[n])], w=[kyac(cc, n)])
            si_w, wv = load_w(w_out_d[l, row0:row0 + 256, :], 2, 1024)
            for kc in range(2):
                gc = PO_GOUT + 8 * l + row0 // 128 + kc
                P.add("dve", lambda e, kc=kc, gc=gc, wv=wv: e.tensor_scalar(
                    out=wv[:, kc, :], in0=wv[:, kc, :], scalar1=pcol(gc), scalar2=None, op0=ALU.mult),
                    r=[kw(si_w), "prm"], w=[kw(si_w)])
            if mid is not None:
                en = P.enabled
                mid()
                P.enabled = en
            for m in range(8):
                for n in range(4):
                    bi = brot.next()

                    def f(e, m=m, n=n, bi=bi, wv=wv):
                        e.matmul(bank(bi), lhsT=wv[:, 0, m * 128:(m + 1) * 128], rhs=yac[:, 0, 512 * n:512 * (n + 1)],
                                 start=True, stop=False)
                        return e.matmul(bank(bi), lhsT=wv[:, 1, m * 128:(m + 1) * 128],
                                        rhs=yac[:, 1, 512 * n:512 * (n + 1)], start=False, stop=True)
                    P.add("pe", f, r=[kyac(0, n), kyac(1, n), kw(si_w)], w=[kb(bi)])
                    P.add("dve", lambda e, m=m, n=n, bi=bi: e.tensor_tensor(
                        out=xT[:, m, 512 * n:512 * (n + 1)], in0=xT[:, m, 512 * n:512 * (n + 1)], in1=bank(bi),
                        op=ALU.add), r=[kb(bi), kx(m, n)], w=[kx(m, n)])

        UW = 30 + S
        u3 = qkv_t[:, 0:2 * UW].rearrange("p (c t) -> p c t", c=2)
        def ku(cc, t0, nt): return KR("Q", (cc * UW + t0) * 2, nt * 2)

        def c_head():
            P.enabled = "C" in phases
            for cc in range(2):
                P.add("dve", lambda e, cc=cc: e.memset(u3[:, cc, 0:30], 0.0), w=ku(cc, 0, 30))
            sC = {}
            for name, col in (("val", 2304), ("gate", 2560)):
                sC[name] = load_w(w_in_d[l, :, col:col + 256], 8, 256)
            for cc in range(2):
                for n in range(4):
                    si_g, wg = sC["gate"]
                    si_v, wvv = sC["val"]
                    bg_, bv_ = brot.next(), brot.next()
                    P.add("pe", mm8(wg, cc * 128, hrhs, n, bg_), r=KHN(n) + [kw(si_g)], w=[kb(bg_)])
                    P.add("pe", mm8(wvv, cc * 128, hrhs, n, bv_), r=KHN(n) + [kw(si_v)], w=[kb(bv_)])
                    si = smrot.next()
                    P.add("act", lambda e, si=si, b=bg_: e.activation(out=sm(si), in_=bank(b), func=AF.Tanh, scale=0.5),
                          r=[kb(bg_)], w=[ksm(si)])
                    P.add("dve", lambda e, si=si, b=bv_, n=n, cc=cc: e.scalar_tensor_tensor(
                        out=u3[:, cc, 30 + 512 * n:30 + 512 * (n + 1)], in0=sm(si), scalar=1.0, in1=bank(b),
                        op0=ALU.add, op1=ALU.mult), r=[ksm(si), kb(bv_)], w=ku(cc, 30 + 512 * n, 512))

        P.enabled = "A" in phases
        ssA = [brot.next() for _ in range(4)]
        for n in range(4):
            for cc in range(2):
                qi = sqrot.next()
                P.add("act", lambda e, n=n, qi=qi, cc=cc: e.activation(out=sq(qi), in_=yac[:, cc, 512 * n:512 * (n + 1)],
                                                                       func=AF.Square), r=[kyac(cc, n)], w=[ksq(qi)])
                P.add("pe", lambda e, bk=ssA[n], qi=qi, cc=cc: e.matmul(bank(bk), lhsT=onesB_t[:], rhs=sq(qi),
                                                                      start=(cc == 0), stop=(cc == 1)),
                      r=[ksq(qi), "onesB"], w=[kb(ssA[n])])
        finish_group(ssA, 1.0 / 256, 0, 1.0, mid=c_head)
        NPT = 6
        PTW = 1024
        pt_v = hT_t[:, 0:NPT * PTW]
        OSB0 = NPT * PTW * 2
        osb_v = hT_t[:, NPT * PTW:NPT * PTW + 6 * 1024].bitcast(F32)
        ptrot = Rot(range(NPT))
        vtrot = Rot(range(4))

        def pt4(i): return pt_v[:, PTW * i:PTW * (i + 1)].rearrange("p (h i q) -> p h i q", h=2, i=2)
        def kpt(i, hh): return KR("H", PTW * 2 * i + 1024 * hh, 1024)
        def vt4(i): return vt_t[:, 260 * i:260 * (i + 1)].rearrange("p (i h d) -> p i h d", i=2, h=2)
        def osb(hh): return osb_v[:, 1024 * hh:1024 * (hh + 1)]
        def kosb(hh, j): return KR("H", OSB0 + hh * 4096 + j * 2048, 2048)
        rbc_v = osb_v[:, 2048:3072]
        K_RD = KR("H", OSB0 + 8192, 4096)
        VPW = 16 * 2 * 128
        VP2_EL = NPT * PTW + 6 * 1024
        assert VP2_EL + VPW <= 8 * S

        def vp_flat(b_):
            if b_ < 2:
                return qkv_t[:, 8 * S + VPW * b_:8 * S + VPW * (b_ + 1)]
            return hT_t[:, VP2_EL:VP2_EL + VPW]
        def vp4(b_): return vp_flat(b_).rearrange("p (t h d) -> p t h d", t=16, h=2)
        def kvp(b_): return KR("Q", 32768 + VPW * 2 * b_, VPW * 2) if b_ < 2 else KR("H", VP2_EL * 2, VPW * 2)
        VP_ALL = [("vp", b_, g_, h_) for b_ in range(3) for g_ in range(4) for h_ in range(2)]
        VP_01 = [k_ for k_ in VP_ALL if k_[1] < 2]
        VP_2 = [k_ for k_ in VP_ALL if k_[1] == 2]
        VSCR_ALL = [("vscr", tt) for tt in range(16)]

        def load_vp(p, b_):
            cols = slice(128 * p, 128 * (p + 1))
            if b_ == 0:
                src = vscr_d[:, cols].rearrange("(t q) c -> q t c", q=128)
            elif b_ == 1:
                src = vscr_d[:, cols].rearrange("(n q r) c -> q r n c", q=128, r=4)
            else:
                src = vscr_d[:, cols].rearrange("(q r) c -> q r c", r=16)
            for t0 in range(0, 16, 4):
                if b_ == 1:
                    sap = src[:, t0 // 4, :, :]
                else:
                    sap = src[:, t0:t0 + 4, :]
                for hh in range(2):
                    dap = vp4(b_)[:, t0:t0 + 4, hh, 0:64]
                    P.add("sp", lambda e, sap=sap, dap=dap, hh=hh: e.dma_start(out=dap, in_=sap[:, :, 64 * hh:64 * (hh + 1)]),
                          r=VSCR_ALL, w=[("vp", b_, t0 // 4, hh)], dma=True)
        P.enabled = "Q" in phases
        vsl = [load_w(w_in_d[l, :, 1792 + 256 * sl_:1792 + 256 * (sl_ + 1)], 8, 256) for sl_ in range(2)]
        for tt in range(16):
            bi = brot.next()

            def fv(e, tt=tt, bi=bi, vsl=vsl):
                ins = None
                for sl_ in range(2):
                    wv = vsl[sl_][1]
                    for kc in range(8):
                        ins = e.matmul(bank(bi)[:, 256 * sl_:256 * (sl_ + 1)], lhsT=hT[:, kc, 128 * tt:128 * (tt + 1)],
                                       rhs=wv[:, kc, :], start=(kc == 0), stop=(kc == 7), skip_group_check=True)
                return ins
            P.add("pe", fv, r=KHN(tt // 4) + [kw(vsl[0][0]), kw(vsl[1][0])], w=[kb(bi)])
            vst = yac_t[:, 512 * (tt % 8):512 * (tt % 8 + 1)]
            kvst = [("Y", tt % 8)]
            if evrot.next() == "act":
                P.add("act", lambda e, vst=vst, bi=bi: e.copy(out=vst, in_=bank(bi)), r=[kb(bi)], w=kvst)
            else:
                P.add("dve", lambda e, vst=vst, bi=bi: e.tensor_copy(out=vst, in_=bank(bi)), r=[kb(bi)], w=kvst)
            P.add("sp", lambda e, vst=vst, tt=tt: e.dma_start(out=vscr_d[128 * tt:128 * (tt + 1), :], in_=vst),
                  r=kvst, w=[("vscr", tt)], dma=True)
        P.add("pool", lambda e: e.memset(qkv_t[:, 8 * S:8 * S + 2 * VPW], 1.0), w=kvp(0) + kvp(1) + VP_01)
        load_vp(0, 0)
        load_vp(0, 1)
        P.enabled = "C" in phases
        brot = Rot(range(4))
        convb = [4, 5, 6, 7]
        for cc in range(2):
            for k in range(31):
                di = dgrot.next()
                wc = PO_CVC + (l * 31 + k) * 2 + cc
                P.add("dve", lambda e, di=di, wc=wc: e.tensor_scalar(
                    out=dg_t[:, 128 * di:128 * (di + 1)], in0=identB_t[:], scalar1=pcol(wc), scalar2=0.5,
                    op0=ALU.mult, op1=ALU.mult), r=["identB", "prm"], w=[("dg", di)])
                for n in range(4):
                    P.add("pe", lambda e, di=di, n=n, k=k, cc=cc: e.matmul(
                        bank(convb[n]), lhsT=dg_t[:, 128 * di:128 * (di + 1)], rhs=u3[:, cc, 512 * n + k:512 * n + k + 512],
                        start=(k == 0), stop=(k == 30)), r=ku(cc, 512 * n + k, 512) + [("dg", di)], w=[kb(convb[n])])
            for n in range(4):
                bc_ = PO_CVCB + 2 * l + cc
                P.add("act", lambda e, n=n, cc=cc, bc_=bc_: e.activation(
                    out=yac[:, cc, 512 * n:512 * (n + 1)], in_=bank(convb[n]), func=AF.Identity, bias=pcol(bc_), scale=1.0),
                    r=[kb(convb[n]), "prm"], w=[kyac(cc, n)])
        def qkv_units():
            for which, col0, dst4, scl in (("q", 768, q4, 0.125), ("k", 1280, k4, 1.0)):
                for sl in range(2):
                    si_w, wv = load_w(w_in_d[l, :, col0 + 256 * sl:col0 + 256 * (sl + 1)], 8, 256)
                    for j in range(2):
                        p = sl * 2 + j
                        for n in range(4):
                            bi = brot.next()
                            P.add("pe", mm8(wv, j * 128, hrhs, n, bi), r=KHN(n) + [kw(si_w)], w=[kb(bi)])
                            dst = dst4[:, p, 512 * n:512 * (n + 1)]
                            if evrot.next() == "act":
                                P.add("act", lambda e, dst=dst, bi=bi, scl=scl: e.activation(out=dst, in_=bank(bi), func=AF.Copy,
                                                                                             scale=scl),
                                      r=[kb(bi)], w=[kqkv(which, p, n)])
                            else:
                                P.add("dve", lambda e, dst=dst, bi=bi, scl=scl: e.tensor_scalar(
                                    out=dst, in0=bank(bi), scalar1=scl, scalar2=None, op0=ALU.mult),
                                    r=[kb(bi)], w=[kqkv(which, p, n)])
                            yield
        qkv_gen = qkv_units()

        def qkv_emit(k):
            en = P.enabled
            P.enabled = "Q" in phases
            for _ in range(k):
                next(qkv_gen, None)
            P.enabled = en

        ssC = [4, 5, 6, 7]
        pend_ss = []
        for n in range(4):
            b1, b2 = brot.next(), brot.next()
            for cc in range(2):
                P.add("pe", lambda e, n=n, cc=cc, b1=b1: e.matmul(bank(b1), lhsT=onesB_t[:], rhs=yac[:, cc, 512 * n:512 * (n + 1)],
                                                                  start=(cc == 0), stop=(cc == 1)),
                      r=[kyac(cc, n), "onesB"], w=[kb(b1)])
                qi = sqrot.next()
                P.add("act", lambda e, n=n, cc=cc, qi=qi: e.activation(out=sq(qi), in_=yac[:, cc, 512 * n:512 * (n + 1)],
                                                                       func=AF.Square), r=[kyac(cc, n)], w=[ksq(qi)])
                P.add("pe", lambda e, cc=cc, b2=b2, qi=qi: e.matmul(bank(b2), lhsT=onesB_t[:], rhs=sq(qi),
                                                                    start=(cc == 0), stop=(cc == 1)),
                      r=[ksq(qi), "onesB"], w=[kb(b2)])
            s_m, s_v = smrot.next(), smrot.next()
            P.add("dve", lambda e, s_m=s_m, b1=b1: e.tensor_scalar(out=sm(s_m), in0=bank(b1), scalar1=1.0 / 256,
                                                                   scalar2=None, op0=ALU.mult),
                  r=[kb(b1)], w=[ksm(s_m)])
            P.add("dve", lambda e, s_m=s_m, s_v=s_v: e.tensor_tensor(out=sm(s_v), in0=sm(s_m), in1=sm(s_m), op=ALU.mult),
                  r=[ksm(s_m)], w=[ksm(s_v)])
            P.add("dve", lambda e, s_v=s_v, b2=b2: e.scalar_tensor_tensor(
                out=sm(s_v), in0=bank(b2), scalar=1.0 / 256, in1=sm(s_v), op0=ALU.mult, op1=ALU.subtract),
                r=[kb(b2), ksm(s_v)], w=[ksm(s_v)])
            def frs(e, s_v=s_v):
                e.activation(out=sm(s_v), in_=sm(s_v), func=AF.Ln, bias=EPS, scale=1.0)
                return e.activation(out=sm(s_v), in_=sm(s_v), func=AF.Exp, scale=-0.5)
            P.add("act", frs, r=[ksm(s_v)], w=[ksm(s_v)])
            for cc in range(2):
                s_d, s_t = smrot.next(), smrot.next()
                gcol = PO_LNG + 2 * l + cc
                bcol = PO_LNB + 2 * l + cc
                hbcol = PO_HLNB + 2 * l + cc

                def fz(e, n=n, cc=cc, s_d=s_d, s_m=s_m, s_v=s_v, gcol=gcol):
                    e.tensor_tensor(out=sm(s_d), in0=yac[:, cc, 512 * n:512 * (n + 1)], in1=sm(s_m), op=ALU.subtract)
                    return e.scalar_tensor_tensor(out=sm(s_d), in0=sm(s_d), scalar=pcol(gcol), in1=sm(s_v),
                                                  op0=ALU.mult, op1=ALU.mult)
                P.add("dve", fz, r=[kyac(cc, n), ksm(s_m), ksm(s_v), "prm"], w=[ksm(s_d)])
                P.add("act", lambda e, n=n, cc=cc, s_d=s_d, bcol=bcol: e.activation(
                    out=yac[:, cc, 512 * n:512 * (n + 1)], in_=sm(s_d), func=AF.Silu, bias=pcol(bcol), scale=1.0),
                    r=[ksm(s_d), "prm"], w=[kyac(cc, n)])
                qi = sqrot.next()
                P.add("act", lambda e, n=n, cc=cc, qi=qi: e.activation(out=sq(qi), in_=yac[:, cc, 512 * n:512 * (n + 1)],
                                                                       func=AF.Square), r=[kyac(cc, n)], w=[ksq(qi)])
                pend_ss.append((n, cc, qi))
            qkv_emit(7)
            for (n_, cc_, qi_) in pend_ss:
                P.add("pe", lambda e, n=n_, cc=cc_, qi=qi_: e.matmul(bank(ssC[n]), lhsT=onesB_t[:], rhs=sq(qi),
                                                                     start=(cc == 0), stop=(cc == 1)),
                      r=[ksq(qi_), "onesB"], w=[kb(ssC[n_])])
            pend_ss = []
        finish_group(ssC, 1.0 / 256, 768, 1.0, mid=lambda: qkv_emit(48))
        brot = Rot(range(8))

        P.enabled = "Q" in phases
        qkv_emit(48)
        brot = Rot(range(8))

        P.enabled = "T" in phases
        P.add("pool", lambda e: e.memset(hT_t[:, VP2_EL:VP2_EL + VPW], 1.0), w=kvp(2) + VP_2)
        load_vp(0, 2)
        XB = {0: 4, 1: 5}

        deferred = []
        pq = []
        gcount = [0]

        def pq_drain(limit):
            while sum(1 for k_, _ in pq if k_ == "pv") > limit:
                pq.pop(0)[1]()
            while pq and pq[0][0] != "pv":
                pq.pop(0)[1]()
        for p in range(4):
            for hf in range(2):
                items = attn_items(hf)
                groups = []
                for it in items:
                    sig = (it[0], it[4], it[6], it[7])
                    if groups and len(groups[-1]) < 2 and groups[-1][0][1] == sig:
                        groups[-1].append((it, sig))
                    else:
                        groups.append([(it, sig)])
                started = set()
                LAG = 4

                def emit_pv(grp, pi, vi, hf=hf, started=started):
                    for ii, (it, sig) in enumerate(grp):
                        (bi_, d, r, ki0, kp, qi0, nq, qoff) = it
                        vtile = (ki0 // 128) if d == 1 else ((r * 4 + ki0 // 128) if d == 4 else r)
                        per = 512 // d
                        s0 = qi0
                        while s0 < qi0 + nq:
                            e0 = min(qi0 + nq, (s0 // per + 1) * per)
                            col = s0 * d + r - 1024 * hf
                            bsub = col // 512
                            for hh in range(2):
                                ab = 2 * hh + bsub
                                first = ab not in started
                                started.add(ab)
                                c0 = col - 512 * bsub
                                cnt = e0 - s0
                                a_, b_ = s0 - qi0, e0 - qi0
                                if d == 1:
                                    oap = bank(ab)[0:128, :].rearrange("p (r i) -> p r i", r=4)[:, :, c0 // 4:(c0 + cnt) // 4]
                                    rap = pt4(pi)[0:kp, hh, ii, a_:b_].rearrange("p (i r) -> p r i", r=4)
                                elif d == 4:
                                    il0 = (c0 - r) // 4
                                    oap = bank(ab)[0:128, r * 128 + il0:r * 128 + il0 + cnt]
                                    rap = pt4(pi)[0:kp, hh, ii, a_:b_]
                                else:
                                    r4, a4 = r % 4, r // 4
                                    il0 = (c0 - r4) // 4
                                    oap = bank(ab)[0:128, r4 * 128 + il0:r4 * 128 + il0 + (cnt - 1) * 4 + 1:4]
                                    rap = pt4(pi)[0:kp, hh, ii, a_:b_]

                                def fpv(e, oap=oap, rap=rap, bi_=bi_, vtile=vtile, hh=hh, kp=kp, first=first):
                                    return e.matmul(oap, lhsT=vp4(bi_)[0:kp, vtile, hh, 0:128], rhs=rap,
                                                    start=first, stop=False, skip_group_check=True)
                                P.add("pe", fpv, r=kpt(pi, hh) + kvp(bi_) + [("vp", bi_, vtile // 4, hh)], w=[kb(ab)])
                            s0 = e0

                for gi, grp in enumerate(groups):
                    ng = len(grp)
                    (bi_, kp, nq, qoff) = grp[0][1]
                    d = grp[0][0][1]
                    sl = []
                    for (it, sig) in grp:
                        (_, _, r, ki0, _, qi0, _, _) = it
                        kt0 = ki0 * d + r
                        kt1 = kt0 + (kp - 1) * d + 1
                        qt0 = qi0 * d + r
                        qt1 = qt0 + (nq - 1) * d + 1
                        sl.append((slice(kt0, kt1, d), slice(qt0, qt1, d), kt0, kt1, qt0, qt1))
                    vi = 0
                    pi = ptrot.next()
                    xb0 = 4 + 2 * (gcount[0] % 2)
                    gcount[0] += 1

                    def fst(e, sl=sl, kp=kp, nq=nq, p=p, xb0=xb0):
                        ins = None
                        for ii, s_ in enumerate(sl):
                            for hh in range(2):
                                ins = e.matmul(bank(xb0 + hh)[0:kp, 256 * ii:256 * ii + nq],
                                               lhsT=k4[64 * hh:64 * (hh + 1), p, s_[0]],
                                               rhs=q4[64 * hh:64 * (hh + 1), p, s_[1]], start=True, stop=True,
                                               skip_group_check=True)
                        return ins
                    kkr = [k_ for s_ in sl for k_ in kqkv_rng("k", p, s_[2], s_[3]) + kqkv_rng("q", p, s_[4], s_[5])]
                    P.add("pe", fst, r=kkr, w=[kb(xb0), kb(xb0 + 1)])
                    ec = E_OFF[bi_] + qoff
                    xin = ps_t[:, xb0 * 512:(xb0 + 2) * 512].rearrange("p (h i q) -> p h i q", h=2, i=2)[0:kp, :, 0:ng, 0:nq]
                    P.add("act", lambda e, pi=pi, kp=kp, nq=nq, ng=ng, xin=xin: e.activation(
                        out=pt4(pi)[0:kp, :, 0:ng, 0:nq], in_=xin, func=AF.Exp),
                        r=[kb(xb0), kb(xb0 + 1)], w=kpt(pi, 0) + kpt(pi, 1))
                    P.add("dve", lambda e, pi=pi, kp=kp, nq=nq, ng=ng, ec=ec, p=p: e.tensor_tensor(
                        out=pt4(pi)[0:kp, :, 0:ng, 0:nq], in0=pt4(pi)[0:kp, :, 0:ng, 0:nq],
                        in1=E3[0:kp, 2 * p:2 * p + 2, ec:ec + nq].unsqueeze(2).broadcast_to([kp, 2, ng, nq]), op=ALU.mult),
                        r=kpt(pi, 0) + kpt(pi, 1) + ["E"], w=kpt(pi, 0) + kpt(pi, 1))
                    pq.append(("pv", lambda f=emit_pv, grp=grp, pi=pi: f(grp, pi, 0)))
                    last_of_branch = (gi + 1 == len(groups)) or (groups[gi + 1][0][1][0] != bi_)
                    if hf == 1 and p + 1 < 4 and last_of_branch:
                        pq.append(("misc", lambda p=p, b_=bi_: load_vp(p + 1, b_)))
                    pq_drain(LAG)
                    if gi >= 1 and deferred:
                        deferred.pop(0)()

                def mk_post(p=p, hf=hf):
                    steps = []
                    for hh in range(2):
                        def s_rd(hh=hh):
                            def frd(e, hh=hh):
                                e.activation(out=rbc_v[0:64, :], in_=osb(hh)[64:128, :], func=AF.Ln)
                                return e.activation(out=rbc_v[0:64, :], in_=rbc_v[0:64, :], func=AF.Exp, scale=-1.0)
                            P.add("act", frd, r=kosb(hh, 0) + kosb(hh, 1), w=K_RD)
                        steps.append(s_rd)
                        for j in range(2):
                            def s_y(hh=hh, j=j, p=p, hf=hf):
                                n = 2 * hf + j
                                P.add("dve", lambda e, hh=hh, j=j, n=n, p=p: e.tensor_tensor(
                                    out=q4[64 * hh:64 * (hh + 1), p, 512 * n:512 * (n + 1)],
                                    in0=osb(hh)[0:64, 512 * j:512 * (j + 1)], in1=rbc_v[0:64, 512 * j:512 * (j + 1)], op=ALU.mult),
                                    r=kosb(hh, j) + K_RD, w=[kqkv("q", p, n)])
                            steps.append(s_y)
                    return steps
                def block_end(mk_post=mk_post):
                    while deferred:
                        deferred.pop(0)()
                    for hh in range(2):
                        P.add("dve", lambda e, hh=hh: e.tensor_copy(
                            out=osb(hh)[:, 0:512].rearrange("p (i r) -> p i r", r=4),
                            in_=bank(2 * hh)[:, :].rearrange("p (r i) -> p i r", r=4)),
                              r=[kb(2 * hh)], w=kosb(hh, 0))
                        P.add("dve", lambda e, hh=hh: e.tensor_copy(
                            out=osb(hh)[:, 512:1024].rearrange("p (i r) -> p i r", r=4),
                            in_=bank(2 * hh + 1)[:, :].rearrange("p (r i) -> p i r", r=4)),
                              r=[kb(2 * hh + 1)], w=kosb(hh, 1))
                    deferred.extend(mk_post())
                pq.append(("misc", block_end))
        pq_drain(0)
        while deferred:
            deferred.pop(0)()

        P.enabled = "B" in phases
        si_w0, wv0 = load_w(w_out_d[l, 256:512, :], 2, 1024)
        si_w1, wv1 = load_w(w_out_d[l, 512:768, :], 2, 1024)
        for kc4 in range(4):
            si_w, wv = (si_w0, wv0) if kc4 < 2 else (si_w1, wv1)
            gc = PO_GOUT + 8 * l + 2 + kc4
            P.add("dve", lambda e, kc=kc4 % 2, gc=gc, wv=wv: e.tensor_scalar(
                out=wv[:, kc, :], in0=wv[:, kc, :], scalar1=pcol(gc), scalar2=None, op0=ALU.mult),
                r=[kw(si_w), "prm"], w=[kw(si_w)])
        for n in range(4):
            bi = brot.next()
            for pp in range(4):
                qi = sqrot.next()
                P.add("act", lambda e, n=n, pp=pp, qi=qi: e.activation(out=sq(qi), in_=q4[:, pp, 512 * n:512 * (n + 1)],
                                                                       func=AF.Square),
                      r=[kqkv("q", pp, n)], w=[ksq(qi)])
                P.add("pe", lambda e, pp=pp, qi=qi, bi=bi: e.matmul(bank(bi), lhsT=onesB_t[:], rhs=sq(qi),
                                                                    start=(pp == 0), stop=(pp == 3)),
                      r=[ksq(qi), "onesB"], w=[kb(bi)])
            rs = rstd_from_ss(bi, 1.0 / 512)
            for m in range(8):
                b2 = brot.next()

                def fo(e, m=m, n=n, b2=b2, wv0=wv0, wv1=wv1):
                    ins = None
                    for pp in range(4):
                        wv = wv0 if pp < 2 else wv1
                        ins = e.matmul(bank(b2), lhsT=wv[:, pp % 2, m * 128:(m + 1) * 128],
                                       rhs=q4[:, pp, 512 * n:512 * (n + 1)], start=(pp == 0), stop=(pp == 3))
                    return ins
                P.add("pe", fo, r=[kqkv("q", pp, n) for pp in range(4)] + [kw(si_w0), kw(si_w1)], w=[kb(b2)])
                st = (rs + 1 + (m % 3)) % NSM
                P.add("dve", lambda e, b2=b2, rs=rs, st=st: e.tensor_tensor(out=sm(st), in0=bank(b2), in1=sm(rs), op=ALU.mult),
                      r=[kb(b2), ksm(rs)], w=[ksm(st)])
                P.add("dve", lambda e, m=m, n=n, st=st: e.tensor_tensor(
                    out=xT[:, m, 512 * n:512 * (n + 1)], in0=xT[:, m, 512 * n:512 * (n + 1)], in1=sm(st), op=ALU.add),
                    r=[ksm(st), kx(m, n)], w=[kx(m, n)])

        P.enabled = "F" in phases
        act3 = qkv_t[:, 0:22 * 1024].rearrange("p (j t) -> p j t", j=22)
        def kact(j, n2=None):
            return KR("Q", j * 2048, 2048) if n2 is None else KR("Q", j * 2048 + n2 * 1024, 1024)
        h2 = hT_t[:, 0:8 * 1024].rearrange("p (c t) -> p c t", c=8)
        def kh2(c, n2): return KR("H", (c * 1024 + n2 * 512) * 2, 1024)
        ft_v = hT_t[:, 8 * 1024:16 * 1024].bitcast(F32)
        ft2_v = yac_t[:].bitcast(F32)
        ftiles = [ft_v[:, 1024 * i:1024 * (i + 1)] for i in range(4)] + [ft2_v[:, 1024 * i:1024 * (i + 1)] for i in range(2)]
        def kft(i): return KR("H", 16384 + 4096 * i, 4096) if i < 4 else KR("Y", 4096 * (i - 4), 4096)

        def h2rhs(kc, n2):
            return h2[:, kc, 512 * n2:512 * (n2 + 1)]
        brot = Rot(range(8))
        for hf in range(2):
            if hf == 0:
                rmsnorm_tiles(PO_GFFN + 8 * l, [0, 1],
                              lambda c, n: (h2[:, c, 512 * (n % 2):512 * (n % 2 + 1)], kh2(c, n % 2)))
            ftrot = Rot([0, 1, 2, 3, 4, 5])
            dn = {}
            def load_dn(m, part):
                dn[(m, part)] = load_w(w_dn_d[l, 1408 * part:1408 * (part + 1), m * 128:(m + 1) * 128], 11, 128)
            upw = {}
            def load_up(jj):
                upw[jj] = (load_w(w_up_d[l, :, 256 * jj:256 * (jj + 1)], 8, 256),
                           load_w(w_up_d[l, :, DFF + 256 * jj:DFF + 256 * (jj + 1)], 8, 256))
            load_up(0)
            for jj in range(11):
                if jj + 1 < 11:
                    load_up(jj + 1)
                else:
                    load_dn(0, 0)
                    load_dn(1, 0)
                    load_dn(0, 1)
                (si_g, wg), (si_v, wv_) = upw[jj]
                for j2 in range(2):
                    j = 2 * jj + j2
                    R = {}
                    for gv, (si_w, wv) in enumerate(((si_g, wg), (si_v, wv_))):
                        ch = j + 22 * gv
                        fi = ftrot.next()
                        R[gv] = fi
                        Rt = ftiles[fi]
                        cw = PO_CVF + (l * 3) * 44 + ch
                        b0 = brot.next()
                        b1 = brot.next()
                        assert b0 % 2 == 0 and b1 == b0 + 1
                        ps2 = ps_t[:, 512 * b0:512 * (b0 + 2)]
                        for n2, bi in ((0, b0), (1, b1)):
                            P.add("pe", mm8(wv, j2 * 128, h2rhs, n2, bi),
                                  r=[k_ for kc in range(8) for k_ in kh2(kc, n2)] + [kw(si_w)], w=[kb(bi)])
                        P.add("act", lambda e, ps2=ps2, Rt=Rt, cw=cw: e.activation(
                            out=Rt[:, :], in_=ps2, func=AF.Identity, scale=pcol(cw + 88)),
                            r=[kb(b0), kb(b1), "prm"], w=kft(fi))

                        def ftap(e, Rt=Rt, ps2=ps2, cw=cw, ch=ch, hf=hf):
                            w1, w0 = pcol(cw + 44), pcol(cw)
                            stt = e.scalar_tensor_tensor
                            stt(out=Rt[:, 1:1024], in0=ps2[:, 0:1023], scalar=w1, in1=Rt[:, 1:1024], op0=ALU.mult, op1=ALU.add)
                            ins = stt(out=Rt[:, 2:1024], in0=ps2[:, 0:1022], scalar=w0, in1=Rt[:, 2:1024], op0=ALU.mult, op1=ALU.add)
                            hl = halo_t[:, 2 * ch:2 * ch + 2]
                            if hf == 1:
                                stt(out=Rt[:, 0:1], in0=hl[:, 1:2], scalar=w1, in1=Rt[:, 0:1], op0=ALU.mult, op1=ALU.add)
                                ins = stt(out=Rt[:, 0:2], in0=hl[:, 0:2], scalar=w0, in1=Rt[:, 0:2], op0=ALU.mult, op1=ALU.add)
                            else:
                                ins = e.tensor_copy(out=hl, in_=ps2[:, 1022:1024])
                            return ins
                        P.add("dve", ftap, r=[kb(b0), kb(b1), "prm", ("halo", ch)] + kft(fi), w=kft(fi) + [("halo", ch)])
                    fT = ftrot.next()
                    Tt = ftiles[fT]
                    Rg, Rv = ftiles[R[0]], ftiles[R[1]]
                    P.add("act", lambda e, Tt=Tt, Rg=Rg: e.activation(out=Tt, in_=Rg, func=AF.Silu),
                          r=kft(R[0]), w=kft(fT))
                    P.add("pool", lambda e, Tt=Tt, Rv=Rv, j=j: e.tensor_tensor(out=act3[:, j, :], in0=Tt, in1=Rv, op=ALU.mult),
                          r=kft(fT) + kft(R[1]), w=kact(j))
            if hf == 0:
                rmsnorm_tiles(PO_GFFN + 8 * l, [2, 3],
                              lambda c, n: (h2[:, c, 512 * (n % 2):512 * (n % 2 + 1)], kh2(c, n % 2)))
            load_dn(1, 1)

            def fd(e, n2, bi, wv, j0):
                ins = None
                for j in range(j0, j0 + 11):
                    ins = e.matmul(bank(bi), lhsT=wv[:, j - j0, :], rhs=act3[:, j, 512 * n2:512 * (n2 + 1)],
                                   start=(j == 0), stop=(j == 21))
                return ins
            for mp in range(4):
                ms = (2 * mp, 2 * mp + 1)
                bis = {(m, n2): brot.next() for m in ms for n2 in range(2)}
                for m in ms:
                    sd0, wd0 = dn[(m, 0)]
                    for n2 in range(2):
                        P.add("pe", lambda e, n2=n2, bi=bis[(m, n2)], wv=wd0: fd(e, n2, bi, wv, 0),
                              r=[k_ for j in range(11) for k_ in kact(j, n2)] + [kw(sd0)], w=[kb(bis[(m, n2)])])
                if mp + 1 < 4:
                    load_dn(2 * mp + 2, 0)
                    load_dn(2 * mp + 3, 0)
                for m in ms:
                    sd1, wd1 = dn[(m, 1)]
                    for n2 in range(2):
                        P.add("pe", lambda e, n2=n2, bi=bis[(m, n2)], wv=wd1: fd(e, n2, bi, wv, 11),
                              r=[k_ for j in range(11, 22) for k_ in kact(j, n2)] + [kw(sd1)], w=[kb(bis[(m, n2)])])
                        n = 2 * hf + n2
                        P.add("dve", lambda e, m=m, n=n, bi=bis[(m, n2)]: e.tensor_tensor(
                            out=xT[:, m, 512 * n:512 * (n + 1)], in0=xT[:, m, 512 * n:512 * (n + 1)], in1=bank(bi), op=ALU.add),
                            r=[kb(bis[(m, n2)]), kx(m, n)], w=[kx(m, n)])
                if mp + 1 < 4:
                    load_dn(2 * mp + 2, 1)
                    load_dn(2 * mp + 3, 1)

    P.enabled = True
    if dump:
        dq0_d = nc.dram_tensor("dbg_q0", [128, 12 * S], BF16, kind="ExternalOutput").ap()
        P.add("sp", lambda e: e.dma_start(out=dq0_d[:, :], in_=qkv_t[:, :]), r=KR("Q", 0, 49152), w=[("out", 3)], dma=True)
    fin3 = stgF.rearrange("p (c t) -> p c t", c=8)
    ostg = qkv_t[:].bitcast(F32)[:, 0:8 * 1024].rearrange("p (j d) -> p j d", j=8)
    def kfin(c, t0, nt): return KR("H", (c * 1024 + t0) * 4, nt * 4)
    for half in range(2):
        if final_norm:
            rmsnorm_tiles(PO_GFIN, [2 * half, 2 * half + 1],
                          lambda c, n: (fin3[:, c, 512 * (n % 2):512 * (n % 2 + 1)], kfin(c, 512 * (n % 2), 512)))
        for jt in range(8):
            for cg in range(2):
                bi = brot.next()

                def ftr(e, jt=jt, cg=cg, bi=bi, half=half):
                    ins = None
                    for i in range(4):
                        c = cg * 4 + i
                        if final_norm:
                            src = fin3[:, c, 128 * jt:128 * (jt + 1)]
                        else:
                            src = xT[:, c, 1024 * half + 128 * jt:1024 * half + 128 * (jt + 1)]
                        ins = e.transpose(bank(bi)[:, 128 * i:128 * (i + 1)], src, identF_t[:])
                    return ins
                if final_norm:
                    rr = [k_ for i in range(4) for k_ in kfin(cg * 4 + i, 128 * jt, 128)]
                else:
                    rr = [kx(cg * 4 + i, 2 * half + jt // 4) for i in range(4)]
                P.add("pe", ftr, r=rr + ["identF"], w=[kb(bi)])
                dst = ostg[:, jt, 512 * cg:512 * (cg + 1)]
                kd = KR("Q", (jt * 1024 + 512 * cg) * 4, 2048)
                if evrot.next() == "act":
                    P.add("act", lambda e, dst=dst, bi=bi: e.copy(out=dst, in_=bank(bi)), r=[kb(bi)], w=kd)
                else:
                    P.add("dve", lambda e, dst=dst, bi=bi: e.tensor_copy(out=dst, in_=bank(bi)), r=[kb(bi)], w=kd)
        dsto = out_d[half * 1024:(half + 1) * 1024, :].rearrange("(j p) d -> p j d", p=128)
        P.add("sp", lambda e, dsto=dsto: e.dma_start(out=dsto, in_=ostg), r=KR("Q", 0, 32768),
              w=[("out", half)], dma=True)
    if dump:
        dh_d = nc.dram_tensor("dbg_h", [128, 8 * S], BF16, kind="ExternalOutput").ap()
        P.add("sp", lambda e: e.dma_start(out=dh_d[:, :], in_=hT_t[:, :]), r=KH_ALL, w=[("out", 2)], dma=True)
        P.add("sp", lambda e: None, r=[("out", 0), ("out", 1), ("out", 2), ("out", 3)])
    else:
        P.add("sp", lambda e: None, r=[("out", 0), ("out", 1)])

    P.emit(nc, stack)
    stack.close()
    return nc


def _t5_bucket_np(dist):
    num_buckets, max_distance = 32, 2048
    max_exact = num_buckets // 2
    d_f = np.maximum(dist, 1).astype(np.float32)
    large = max_exact + (np.log(d_f / max_exact) / math.log(max_distance / max_exact)
                         * (num_buckets - max_exact)).astype(np.int32)
    large = np.minimum(large, num_buckets - 1)
    return np.where(dist < max_exact, dist, large)


def _fm(a):
    a = np.asarray(a, np.float32)
    lead = a.shape[:-1]
    c = a.shape[-1] // 128
    a = a.reshape(lead + (c, 128))
    a = np.moveaxis(a, -1, 0)
    return np.ascontiguousarray(a).reshape(128, -1)


def _host_tables(inp):
    prm = np.zeros((128, NPRM), np.float32)
    prm[:, PO_GMIX:PO_GMIX + 32] = _fm(inp["norm_mix_g"])
    prm[:, PO_GFFN:PO_GFFN + 32] = _fm(inp["norm_ffn_g"])
    prm[:, PO_GOUT:PO_GOUT + 32] = _fm(inp["out_norm_g"])
    prm[:, PO_GFIN:PO_GFIN + 8] = _fm(inp["final_g"])
    prm[:, PO_CVA:PO_CVA + 24] = _fm(inp["conv_a_w"])
    prm[:, PO_CVC:PO_CVC + 248] = _fm(inp["conv_c_w"])
    prm[:, PO_CVCB:PO_CVCB + 8] = _fm(inp["conv_c_b"])
    prm[:, PO_LNG:PO_LNG + 8] = _fm(inp["ln_c_g"])
    prm[:, PO_LNB:PO_LNB + 8] = _fm(inp["ln_c_b"])
    prm[:, PO_CVF:PO_CVF + 528] = _fm(inp["conv_f_w"])
    rb = np.asarray(inp["rel_bias"], np.float32)
    jk = np.arange(128)[:, None]
    bg = np.zeros((128, 8, E_W), np.float32)
    mk = np.zeros((128, E_W), np.float32)
    for bi, d in enumerate((1, 4, 16)):
        w = 256 if d != 16 else 128
        rel = np.arange(w)[None, :] - jk
        idx = _t5_bucket_np(np.maximum(rel, 0) * d)
        bg[:, :, E_OFF[bi]:E_OFF[bi] + w] = np.transpose(rb[idx], (0, 2, 1))
        mk[:, E_OFF[bi]:E_OFF[bi] + w] = ((rel >= 0) & (rel <= 128)).astype(np.float32)
    return prm, bg.reshape(128, 8 * E_W), mk


_CACHE = {}


def kernel(**inputs):
    inp = {k: np.asarray(v) for k, v in inputs.items()}
    x = np.ascontiguousarray(inp["x"], dtype=np.float32)
    prm, bg, mk = _host_tables(inp)
    if "nc" not in _CACHE:
        _CACHE["nc"] = build_program(DEPTH)
    nc = _CACHE["nc"]
    shared = {
        "w_in": np.ascontiguousarray(inp["w_in"], dtype=np.float32),
        "w_out": np.ascontiguousarray(inp["w_out"], dtype=np.float32),
        "w_up": np.ascontiguousarray(inp["w_up"], dtype=np.float32),
        "w_down": np.ascontiguousarray(inp["w_down"], dtype=np.float32),
        "prm": prm, "biasg": bg, "mask01": mk,
    }
    in_maps = [dict(shared, x=x[b]) for b in range(NCORES)]
    res = run_bass_kernel_spmd(nc, in_maps, core_ids=list(range(NCORES)))
    return np.stack([np.asarray(r["out"], dtype=np.float32) for r in res.results], axis=0)
```
